# Optimizing a Trainium2 kernel written in Bass

```python
import math
import jax, jax.numpy as jnp
from jax import lax
import numpy as np

D_MODEL = 1024
BATCH = 16
SEQ = 2048
DEPTH = 4

GRID_W = 64
D_FF = 2816
EPS = 1e-6
CONV_WIDTH = 256
CONV_K = 3
SSM_WIDTH = 256
SSM_GROUP = 16
SSM_GROUPS = SSM_WIDTH // SSM_GROUP
SSM_STATE = 64
NA_HEADS = 8
NA_HEAD_DIM = 64
NA_WIDTH = NA_HEADS * NA_HEAD_DIM
NA_ROWS_MAX = 8
NA_COLS = 16
NA_QBLOCK = 16
NA_KBAND = 32
N_COL_BLOCKS = GRID_W // NA_QBLOCK
RPB_ROWS = 2 * NA_ROWS_MAX - 1
RPB_COLS = 2 * NA_COLS - 1
N_BRANCH = 3
SPLIT_SIZES = (CONV_WIDTH, CONV_WIDTH, CONV_WIDTH, SSM_WIDTH, NA_WIDTH, NA_WIDTH, NA_WIDTH, D_MODEL, D_MODEL, D_MODEL)
IN_COLS = 3 * CONV_WIDTH + SSM_WIDTH + 3 * NA_WIDTH + N_BRANCH * D_MODEL

kernel_name = "hybrid_conv_s5_natten_macaron_encoder"


def rmsnorm(x, g):
    x32 = x.astype(jnp.float32)
    y = x32 * lax.rsqrt(jnp.mean(x32 * x32, axis=-1, keepdims=True) + EPS)
    return y.astype(x.dtype) * g


def swiglu(h, w_gate, w_up, w_down):
    return (jax.nn.silu(h @ w_gate) * (h @ w_up)) @ w_down


def short_conv(z, w):
    zp = jnp.pad(z, ((0, 0), (1, 1), (0, 0)))
    return w[0] * zp[:, :-2] + w[1] * zp[:, 1:-1] + w[2] * zp[:, 2:]


def _ssm_combine(e1, e2):
    a1r, a1i, b1r, b1i = e1
    a2r, a2i, b2r, b2i = e2
    return (a2r * a1r - a2i * a1i,
            a2r * a1i + a2i * a1r,
            a2r * b1r - a2i * b1i + b2r,
            a2r * b1i + a2i * b1r + b2i)


def s5_bidirectional(u, lam_re, lam_im, log_dt, b_re, b_im, c_re, c_im, d_skip):
    f32 = jnp.float32
    bsz, l, _ = u.shape
    u32 = u.astype(f32)
    ug = u32.reshape(bsz, l, SSM_GROUPS, SSM_GROUP)
    y = d_skip.astype(f32) * u32
    for direction in range(2):
        lr = jnp.minimum(lam_re[direction].astype(f32), -1e-4)
        li = lam_im[direction].astype(f32)
        dt = jnp.exp(log_dt[direction].astype(f32))[:, None]
        mag = jnp.exp(lr * dt)
        ab_re = mag * jnp.cos(li * dt)
        ab_im = mag * jnp.sin(li * dt)
        den = lr * lr + li * li
        nr, ni = ab_re - 1.0, ab_im
        f_re = (nr * lr + ni * li) / den
        f_im = (ni * lr - nr * li) / den
        br, bi = b_re[direction].astype(f32), b_im[direction].astype(f32)
        bb_re = f_re[..., None] * br - f_im[..., None] * bi
        bb_im = f_re[..., None] * bi + f_im[..., None] * br
        bu_re = jnp.einsum('blgh,gph->blgp', ug, bb_re)
        bu_im = jnp.einsum('blgh,gph->blgp', ug, bb_im)
        a_re = jnp.broadcast_to(ab_re[None, None], (1, l, SSM_GROUPS, SSM_STATE))
        a_im = jnp.broadcast_to(ab_im[None, None], (1, l, SSM_GROUPS, SSM_STATE))
        _, _, s_re, s_im = lax.associative_scan(
            _ssm_combine, (a_re, a_im, bu_re, bu_im), axis=1, reverse=(direction == 1))
        y_dir = (jnp.einsum('blgp,ghp->blgh', s_re, c_re[direction].astype(f32))
                 - jnp.einsum('blgp,ghp->blgh', s_im, c_im[direction].astype(f32)))
        y = y + y_dir.reshape(bsz, l, SSM_WIDTH)
    return y.astype(u.dtype)


def _na_column_tables():
    qcol = np.arange(GRID_W).reshape(N_COL_BLOCKS, NA_QBLOCK)
    band0 = np.clip(qcol[:, 0] - NA_COLS // 2, 0, GRID_W - NA_KBAND)
    kcol = band0[:, None] + np.arange(NA_KBAND)
    cs = np.clip(qcol - NA_COLS // 2, 0, GRID_W - NA_COLS)
    kc = kcol[:, None, :]
    valid = (kc >= cs[..., None]) & (kc < cs[..., None] + NA_COLS)
    dc_idx = np.clip(kc - qcol[..., None], -(NA_COLS - 1), NA_COLS - 1) + (NA_COLS - 1)
    return kcol, valid, dc_idx


def neighbourhood_attention(q, k, v, rpb):
    b, l = q.shape[0], q.shape[1]
    rows = l // GRID_W
    kr = min(NA_ROWS_MAX, rows)
    shp = (b, rows, GRID_W, NA_HEADS, NA_HEAD_DIM)
    q = q.reshape(shp) * (NA_HEAD_DIM ** -0.5)
    k = k.reshape(shp)
    v = v.reshape(shp)
    kcol, valid, dc_idx = _na_column_tables()
    valid = jnp.asarray(valid)

    def one_row(r):
        rs = jnp.clip(r - kr // 2, 0, rows - kr)
        k_blk = lax.dynamic_slice_in_dim(k, rs, kr, axis=1)[:, :, kcol]
        v_blk = lax.dynamic_slice_in_dim(v, rs, kr, axis=1)[:, :, kcol]
        q_r = lax.dynamic_index_in_dim(q, r, axis=1, keepdims=False)
        q_r = q_r.reshape(b, N_COL_BLOCKS, NA_QBLOCK, NA_HEADS, NA_HEAD_DIM)
        s = jnp.einsum('bnqhd,brnkhd->bhnqrk', q_r, k_blk).astype(jnp.float32)
        dr_idx = rs + jnp.arange(kr) - r + (NA_ROWS_MAX - 1)
        bias = rpb[:, dr_idx[None, None, :, None], dc_idx[:, :, None, :]]
        s = jnp.where(valid[:, :, None, :], s + bias.astype(jnp.float32), -1e30)
        shp_s = s.shape
        p = jax.nn.softmax(s.reshape(shp_s[:4] + (kr * NA_KBAND,)), axis=-1)
        p = p.reshape(shp_s).astype(v.dtype)
        o = jnp.einsum('bhnqrk,brnkhd->bnqhd', p, v_blk)
        return o.reshape(b, GRID_W, NA_WIDTH)

    out = lax.map(one_row, jnp.arange(rows))
    return jnp.moveaxis(out, 0, 1).reshape(b, l, NA_WIDTH)


def setup_inputs(seed: int = 0) -> dict:
    key = jax.random.key(seed)
    ks = iter(jax.random.split(key, 48))
    L, G, P, H = DEPTH, SSM_GROUPS, SSM_STATE, SSM_GROUP

    def nrm(shape, scale):
        return jax.random.normal(next(ks), shape, jnp.float32) * scale

    def gain():
        return 1.0 + nrm((L, D_MODEL), 0.02)

    n = jnp.arange(P, dtype=jnp.float32)
    inp = {}
    inp['x'] = nrm((BATCH, SEQ, D_MODEL), 1.0)
    inp['norm_ffn1_pre'] = gain()
    inp['norm_ffn1_post'] = gain()
    inp['ffn1_w_gate'] = nrm((L, D_MODEL, D_FF), D_MODEL ** -0.5)
    inp['ffn1_w_up'] = nrm((L, D_MODEL, D_FF), D_MODEL ** -0.5)
    inp['ffn1_w_down'] = nrm((L, D_FF, D_MODEL), D_FF ** -0.5)
    inp['norm_mix_pre'] = gain()
    inp['w_in'] = nrm((L, D_MODEL, IN_COLS), D_MODEL ** -0.5)
    inp['conv_w'] = nrm((L, CONV_K, CONV_WIDTH), CONV_K ** -0.5)
    inp['w_out_a'] = nrm((L, CONV_WIDTH, D_MODEL), CONV_WIDTH ** -0.5)
    inp['ssm_lam_re'] = -0.5 + nrm((L, 2, G, P), 0.01)
    inp['ssm_lam_im'] = jnp.pi * n + nrm((L, 2, G, P), 0.01)
    inp['ssm_log_dt'] = jax.random.uniform(next(ks), (L, 2, G), jnp.float32, math.log(1e-3), math.log(1e-1))
    inp['ssm_b_re'] = nrm((L, 2, G, P, H), (2 * H) ** -0.5)
    inp['ssm_b_im'] = nrm((L, 2, G, P, H), (2 * H) ** -0.5)
    inp['ssm_c_re'] = nrm((L, 2, G, H, P), (2 * P) ** -0.5)
    inp['ssm_c_im'] = nrm((L, 2, G, H, P), (2 * P) ** -0.5)
    inp['ssm_d'] = nrm((L, SSM_WIDTH), 1.0)
    inp['w_glu_a'] = nrm((L, SSM_WIDTH, D_MODEL), SSM_WIDTH ** -0.5)
    inp['w_glu_b'] = nrm((L, SSM_WIDTH, D_MODEL), SSM_WIDTH ** -0.5)
    inp['na_rpb'] = nrm((L, NA_HEADS, RPB_ROWS, RPB_COLS), 0.1)
    inp['w_out_c'] = nrm((L, NA_WIDTH, D_MODEL), NA_WIDTH ** -0.5)
    inp['w_o'] = nrm((L, D_MODEL, D_MODEL), D_MODEL ** -0.5)
    inp['norm_mix_post'] = gain()
    inp['norm_ffn2_pre'] = gain()
    inp['norm_ffn2_post'] = gain()
    inp['ffn2_w_gate'] = nrm((L, D_MODEL, D_FF), D_MODEL ** -0.5)
    inp['ffn2_w_up'] = nrm((L, D_MODEL, D_FF), D_MODEL ** -0.5)
    inp['ffn2_w_down'] = nrm((L, D_FF, D_MODEL), D_FF ** -0.5)
    return inp


def reference(x, norm_ffn1_pre, norm_ffn1_post, ffn1_w_gate, ffn1_w_up, ffn1_w_down,
              norm_mix_pre, w_in, conv_w, w_out_a,
              ssm_lam_re, ssm_lam_im, ssm_log_dt, ssm_b_re, ssm_b_im, ssm_c_re, ssm_c_im, ssm_d,
              w_glu_a, w_glu_b, na_rpb, w_out_c, w_o, norm_mix_post,
              norm_ffn2_pre, norm_ffn2_post, ffn2_w_gate, ffn2_w_up, ffn2_w_down):
    b, l, _ = x.shape
    offsets = [int(o) for o in np.cumsum(SPLIT_SIZES)[:-1]]
    for i in range(DEPTH):
        h = rmsnorm(x, norm_ffn1_pre[i])
        x = x + 0.5 * rmsnorm(swiglu(h, ffn1_w_gate[i], ffn1_w_up[i], ffn1_w_down[i]), norm_ffn1_post[i])

        h = rmsnorm(x, norm_mix_pre[i])
        proj = h @ w_in[i]
        a_b, a_c, a_v, s_u, q, k, v, g_a, g_b, g_c = jnp.split(proj, offsets, axis=-1)

        y_a = (a_b * short_conv(a_c * a_v, conv_w[i])) @ w_out_a[i]

        y_s = s5_bidirectional(s_u, ssm_lam_re[i], ssm_lam_im[i], ssm_log_dt[i],
                               ssm_b_re[i], ssm_b_im[i], ssm_c_re[i], ssm_c_im[i], ssm_d[i])
        z = jax.nn.gelu(y_s)
        y_b = (z @ w_glu_a[i]) * jax.nn.sigmoid(z @ w_glu_b[i])

        hs = (b, l, NA_HEADS, NA_HEAD_DIM)
        y_c = neighbourhood_attention(q.reshape(hs), k.reshape(hs), v.reshape(hs), na_rpb[i]) @ w_out_c[i]

        mix = jax.nn.sigmoid(g_a) * y_a + jax.nn.sigmoid(g_b) * y_b + jax.nn.sigmoid(g_c) * y_c
        x = x + rmsnorm(mix @ w_o[i], norm_mix_post[i])

        h = rmsnorm(x, norm_ffn2_pre[i])
        x = x + 0.5 * rmsnorm(swiglu(h, ffn2_w_gate[i], ffn2_w_up[i], ffn2_w_down[i]), norm_ffn2_post[i])
    return x
```

```python
import contextlib
import numpy as np
import concourse.bass as bass
import concourse.mybir as mybir
from concourse.bass_utils import run_bass_kernel_spmd

F32 = mybir.dt.float32
BF16 = mybir.dt.bfloat16
AF = mybir.ActivationFunctionType
ALU = mybir.AluOpType

PE, DVE, ACT, POOL, SP = "pe", "dve", "act", "pool", "sp"
ENGS = [PE, DVE, ACT, POOL, SP]

D = 1024
DFF = 2816
NFC = DFF // 128
NTOK = 4096
NCORES = 8
EPS = 1e-6
PI = 3.14159265358979


_UID = [0]


class Buf:
    __slots__ = ("name", "lw", "rd", "sem", "semcnt", "uid")

    def __init__(self, name):
        self.name = name
        self.lw = None
        self.rd = []
        self.sem = None
        self.semcnt = 0
        _UID[0] += 1
        self.uid = _UID[0]


class Op:
    __slots__ = ("eng", "fn", "waits", "signal", "pos", "val")

    def __init__(self, eng, fn):
        self.eng = eng
        self.fn = fn
        self.waits = []
        self.signal = False
        self.pos = 0
        self.val = 0


class KB:
    def __init__(self, nc, stack):
        self.nc = nc
        self.stack = stack
        self.streams = {e: [] for e in ENGS}
        self.waited = {}
        self.waited_dma = {}
        self.free_sems = []
        self.dma_bufs = []
        self.epoch = 0
        self.nsem = 0

    def sbuf(self, name, shape, dt):
        return self.stack.enter_context(self.nc.sbuf_tensor(name, list(shape), dt))

    def psum(self, name, shape, dt):
        return self.stack.enter_context(self.nc.psum_tensor(name, list(shape), dt))

    def new_sem(self, name):
        self.nsem += 1
        return self.stack.enter_context(self.nc.semaphore(name))

    def _deps(self, reads, writes):
        deps = []
        for b in reads:
            if b.lw is not None:
                deps.append(b.lw)
        for b in writes:
            if b.lw is not None:
                deps.append(b.lw)
            deps.extend(b.rd)
        return deps

    def _add_waits(self, op, deps):
        eng = op.eng
        best = {}
        for d in deps:
            if d[0] == "op":
                o = d[1]
                if o.eng == PE and eng == PE:
                    continue
                k = (eng, o.eng)
                if o.pos <= self.waited.get(k, 0):
                    continue
                if k not in best or best[k].pos < o.pos:
                    best[k] = o
            else:
                _, sem, v, uid, ep = d
                if ep < self.epoch:
                    continue
                k = (eng, id(sem))
                if v <= self.waited_dma.get(k, 0):
                    continue
                self.waited_dma[k] = v
                op.waits.append(d)
        for k, o in best.items():
            self.waited[k] = o.pos
            o.signal = True
            op.waits.append(("op", o))

    def op(self, eng, fn, reads=(), writes=(), deps=()):
        o = Op(eng, fn)
        self.streams[eng].append(o)
        o.pos = len(self.streams[eng])
        self._add_waits(o, self._deps(reads, writes) + list(deps))
        d = ("op", o)
        for b in reads:
            b.rd.append(d)
        for b in writes:
            b.lw = d
            b.rd = []
        return d

    def dma(self, eng, out, in_, reads=(), writes=(), deps=()):
        sb = (list(writes) + list(reads))[0]
        if sb.sem is None:
            if self.free_sems:
                sb.sem, sb.semcnt = self.free_sems.pop()
            else:
                sb.sem = self.new_sem("d%d" % self.nsem)
                sb.semcnt = 0
            self.dma_bufs.append(sb)
        sb.semcnt += 16
        val = sb.semcnt
        sem = sb.sem

        def fn(e):
            return e.dma_start(out=out, in_=in_), sem
        o = Op(eng, fn)
        o.signal = "dma"
        self.streams[eng].append(o)
        o.pos = len(self.streams[eng])
        self._add_waits(o, self._deps(reads, writes) + list(deps))
        d = ("dma", sem, val, sb.uid, self.epoch)
        for b in reads:
            b.rd.append(d)
        for b in writes:
            b.lw = d
            b.rd = []
        return d

    def _all_deps(self):
        deps = []
        for e in ENGS:
            for o in reversed(self.streams[e]):
                if o.fn is not None and o.signal != "dma":
                    deps.append(("op", o))
                    break
        for b in self.dma_bufs:
            deps.append(("dma", b.sem, b.semcnt, b.uid, self.epoch))
        return deps

    def barrier(self):
        deps = self._all_deps()
        for e in ENGS:
            o = Op(e, None)
            self.streams[e].append(o)
            o.pos = len(self.streams[e])
            self._add_waits(o, deps)
        for b in self.dma_bufs:
            self.free_sems.append((b.sem, b.semcnt))
            b.sem = None
        self.dma_bufs = []
        self.epoch += 1

    def emit(self):
        nc = self.nc
        esem = {e: self.new_sem("e_" + e) for e in ENGS}
        fin = Op(SP, None)
        self.streams[SP].append(fin)
        fin.pos = len(self.streams[SP])
        self._add_waits(fin, self._all_deps())
        for e in ENGS:
            c = 0
            for o in self.streams[e]:
                if o.signal is True:
                    c += 1
                    o.val = c

        def run(e, engine):
            for o in self.streams[e]:
                for w in o.waits:
                    if w[0] == "op":
                        engine.wait_ge(esem[w[1].eng], w[1].val)
                    else:
                        engine.wait_ge(w[1], w[2])
                if o.fn is None:
                    continue
                if o.signal == "dma":
                    ins, sem = o.fn(engine)
                    ins.then_inc(sem, 16)
                else:
                    ins = o.fn(engine)
                    if o.signal:
                        ins.then_inc(esem[e], 1)

        with nc.Block() as block:
            @block.sync
            def _(eng):
                run(SP, eng)

            @block.scalar
            def _(eng):
                run(ACT, eng)

            @block.vector
            def _(eng):
                run(DVE, eng)

            @block.gpsimd
            def _(eng):
                run(POOL, eng)

            @block.tensor
            def _(eng):
                run(PE, eng)


def bufs(name, n):
    return [Buf("%s%d" % (name, i)) for i in range(n)]


def _prod(xs):
    r = 1
    for v in xs:
        r *= v
    return r


class Arena:
    def __init__(self, kb, nbytes):
        self.cap = nbytes // 4
        self.t = kb.sbuf("arena", [128, self.cap], F32)
        self.off = 0

    def alloc(self, shape, dt):
        n = _prod(shape[1:])
        words = (n * (2 if dt == BF16 else 4) + 3) // 4
        words = (words + 7) // 8 * 8
        assert self.off + words <= self.cap, ("arena overflow", self.off, words, self.cap)
        v = self.t[:, self.off:self.off + words]
        self.off += words
        self.peak = max(getattr(self, "peak", 0), self.off)
        if dt == BF16:
            v = v.bitcast(BF16)
        v = v[0:shape[0], 0:n]
        if len(shape) == 3:
            v = v.rearrange("p (a b) -> p a b", a=shape[1])
        elif len(shape) == 4:
            v = v.rearrange("p (a b c) -> p a b c", a=shape[1], b=shape[2])
        return v


def mm(kb, out, lhsT, rhs, start, stop, reads, writes):
    return kb.op(PE, lambda e: e.matmul(out, lhsT, rhs, start=start, stop=stop), reads, writes)


def tpose(kb, out, in_, ident, reads, writes):
    return kb.op(PE, lambda e: e.transpose(out, in_, ident), reads, writes)


def act(kb, out, in_, func, reads, writes, scale=None, bias=None, accum=None):
    kw = {}
    if scale is not None:
        kw["scale"] = scale
    if bias is not None:
        kw["bias"] = bias
    if accum is not None:
        kw["accum_out"] = accum
    return kb.op(ACT, lambda e: e.activation(out=out, in_=in_, func=func, **kw), reads, writes)


def tt(kb, eng, out, in0, in1, op, reads, writes):
    return kb.op(eng, lambda e: e.tensor_tensor(out, in0, in1, op), reads, writes)


def ts(kb, eng, out, in0, s1, op0, reads, writes, s2=None, op1=None):
    if op1 is None:
        return kb.op(eng, lambda e: e.tensor_scalar(out=out, in0=in0, scalar1=s1, scalar2=None, op0=op0), reads, writes)
    return kb.op(eng, lambda e: e.tensor_scalar(out=out, in0=in0, scalar1=s1, scalar2=s2, op0=op0, op1=op1), reads, writes)


def stt(kb, out, in0, scalar, in1, op0, op1, reads, writes):
    return kb.op(DVE, lambda e: e.scalar_tensor_tensor(out=out, in0=in0, scalar=scalar, in1=in1, op0=op0, op1=op1), reads, writes)


def cp(kb, eng, out, in_, reads, writes):
    return kb.op(eng, lambda e: e.tensor_copy(out, in_), reads, writes)


def mset(kb, eng, out, val, writes):
    return kb.op(eng, lambda e: e.memset(out, val), (), writes)


def recip(kb, out, in_, reads, writes):
    return kb.op(DVE, lambda e: e.reciprocal(out, in_), reads, writes)


class Ctx:
    def __init__(self, kb, ident, masks=None):
        self.kb = kb
        self.A = Arena(kb, 206 * 1024)
        A = self.A
        self.tp = [kb.psum("tp%d" % i, [128, 1024], BF16) for i in range(2)]
        self.pp = [kb.psum("pp%d" % i, [128, 1024], F32) for i in range(3)]
        self.pb = [self.pp[i // 2][:, (i % 2) * 512:(i % 2) * 512 + 512] for i in range(6)]
        self.identb = A.alloc([128, 128], BF16)
        self.eps = A.alloc([128, 4], F32)
        self.Bconst = Buf("const")
        kb.dma(POOL, self.identb, ident, writes=[self.Bconst])
        mset(kb, DVE, self.eps[:, 0:1], EPS, [self.Bconst])
        mset(kb, DVE, self.eps[:, 1:2], 4.0 * EPS, [self.Bconst])
        mset(kb, DVE, self.eps[:, 2:3], -PI, [self.Bconst])
        if masks is not None:
            self.mk = A.alloc([128, 2, 1024], BF16)
            kb.dma(POOL, self.mk, masks.rearrange("v p n -> p v n"), writes=[self.Bconst])
        self.base = A.off
        self.new_phase()

    def new_phase(self):
        self.A.off = self.base
        self.Btp = bufs("tp", 2)
        self.Bpb = bufs("pb", 6)


def rms_stats(kb, st, Bst, ss_col, out_col, tmp_col, scale, bias_ap, Bconst):
    act(kb, st[:, tmp_col:tmp_col + 1], st[:, ss_col:ss_col + 1], AF.Sqrt, [Bst, Bconst], [Bst], scale=scale, bias=bias_ap)
    recip(kb, st[:, out_col:out_col + 1], st[:, tmp_col:tmp_col + 1], [Bst], [Bst])


def emit_norm_transpose(kb, C, x_in, Bxin, tok0, nsub, gpre, Bgpre, hT, BhT, xs, Bxs, hb, Bhb, st, Bst, cnt):
    for sub in range(nsub):
        s = cnt["x"] % 2; cnt["x"] += 1
        r0 = tok0 + sub * 128
        kb.dma(SP, xs[s], x_in[r0:r0 + 128, :], reads=[Bxin], writes=[Bxs[s]])
        act(kb, hb[s], xs[s], AF.Square, [Bxs[s]], [Bhb[s], Bst[s]], accum=st[s][:, 0:1])
        rms_stats(kb, st[s], Bst[s], 0, 2, 1, 1.0 / D, C.eps[:, 0:1], C.Bconst)
        stt(kb, hb[s], xs[s], st[s][:, 2:3], gpre, ALU.mult, ALU.mult, [Bxs[s], Bst[s], Bgpre], [Bhb[s]])
        for j in range(2):
            p = cnt["tp"] % 2; cnt["tp"] += 1
            for q in range(4):
                dc = j * 4 + q
                tpose(kb, C.tp[p][:, q * 128:(q + 1) * 128], hb[s][:, dc * 128:(dc + 1) * 128], C.identb,
                      [Bhb[s], C.Bconst], [C.Btp[p]])
            src = C.tp[p][:, 0:512].rearrange("p (a b) -> p a b", a=4)
            dst = hT[:, j * 4:(j + 1) * 4, sub * 128:(sub + 1) * 128]
            if j == 0:
                act(kb, dst, src, AF.Copy, [C.Btp[p]], [BhT[sub]])
            else:
                cp(kb, DVE, dst, src, [C.Btp[p]], [BhT[sub]])


def emit_ffn(kb, C, x_in, x_out, gpre_d, gpost_d, wg, wu, wd, Bxin, Bxout, ntok=NTOK, tab=None):
    C.new_phase()
    A = C.A
    TT = 1024
    ntile = ntok // TT
    gpre = A.alloc([128, D], F32); Bgpre = Buf("gpre")
    gpost = A.alloc([128, D], F32); Bgpost = Buf("gpost")
    xs = [A.alloc([128, D], F32) for _ in range(2)]; Bxs = bufs("xs", 2)
    hb = [A.alloc([128, D], BF16) for _ in range(2)]; Bhb = bufs("hb", 2)
    hT = A.alloc([128, 8, TT], BF16); BhT = bufs("hT", 8)
    actt = A.alloc([128, NFC, TT], BF16); Bact = bufs("act", NFC)
    stg = [A.alloc([128, 8, 128], F32) for _ in range(2)]; Bstg = bufs("stg", 2)
    stu = [A.alloc([128, 8, 128], F32) for _ in range(2)]; Bstu = bufs("stu", 2)
    wgb = [A.alloc([128, 8, 128], BF16) for _ in range(2)]; Bwg = bufs("wg", 2)
    wub = [A.alloc([128, 8, 128], BF16) for _ in range(2)]; Bwu = bufs("wu", 2)
    std = [A.alloc([128, D], F32) for _ in range(2)]; Bstd = bufs("std", 2)
    wdb = A.alloc([128, NFC, D], BF16); Bwd = bufs("wd", NFC)
    sg = [A.alloc([128, 512], BF16) for _ in range(2)]; Bsg = bufs("sg", 2)
    ys = [A.alloc([128, D], F32) for _ in range(2)]; Bys = bufs("ys", 2)
    xr = [A.alloc([128, D], F32) for _ in range(2)]; Bxr = bufs("xr", 2)
    st = [A.alloc([128, 8], F32) for _ in range(2)]; Bst = bufs("st", 2)

    kb.dma(SP, gpre, gpre_d.partition_broadcast(128), writes=[Bgpre])
    kb.dma(SP, gpost, gpost_d.partition_broadcast(128), writes=[Bgpost])
    tab_next = 0
    if tab is not None:
        t_par = A.alloc([128, 48], F32); Btpar = Buf("tpar")
        t_prm = A.alloc([128, 24, 16], F32); Btprm = Buf("tprm")
        t_pw = A.alloc([128, 1, 2, 16], F32)
        t_dl = A.alloc([128, 11, 16], F32)
        t_c = A.alloc([128, SEQ], F32); t_s = A.alloc([128, SEQ], F32); Btab = Buf("ttab")
        t_tmp = A.alloc([128, SEQ], F32); Bttmp = Buf("tttmp")
        Btscr = Buf("tabscr")
        kb.dma(SP, t_par, tab[0][:, 0:48], writes=[Btpar])
        emit_mixer_params(kb, C, t_par, Btpar, t_prm, Btprm, t_pw, t_dl, None, None, None, None, None, None, None, None,
                          tables_only=True)

        def all_tables():
            for col in range(16):
                for _ in table_steps(kb, C, col, t_dl, Btprm, t_c, t_s, Btab, t_tmp[:, 0:1024], t_tmp[:, 1024:2048], t_tmp,
                                     Bttmp, tab[1], Btscr):
                    yield 1
        tgen = all_tables()
    for fc in range(NFC):
        s = fc % 2
        kb.dma(SP, std[s], wd[fc * 128:(fc + 1) * 128, :], writes=[Bstd[s]])
        cp(kb, POOL, wdb[:, fc, :], std[s], [Bstd[s]], [Bwd[fc]])
    wg_v = wg.rearrange("(dc dp) f -> dp dc f", dp=128)
    wu_v = wu.rearrange("(dc dp) f -> dp dc f", dp=128)
    cnt = {"x": 0, "w": 0, "gu": 0, "tp": 0, "sg": 0, "y": 0, "py": 0}
    for tt_ in range(ntile):
        t0 = tt_ * TT
        emit_norm_transpose(kb, C, x_in, Bxin, t0, 8, gpre, Bgpre, hT, BhT, xs, Bxs, hb, Bhb, st, Bst, cnt)
        for fc in range(NFC):
            s = cnt["w"] % 2; cnt["w"] += 1
            kb.dma(SP, stg[s], wg_v[:, :, fc * 128:(fc + 1) * 128], writes=[Bstg[s]])
            kb.dma(SP, stu[s], wu_v[:, :, fc * 128:(fc + 1) * 128], writes=[Bstu[s]])
            cp(kb, POOL, wgb[s], stg[s], [Bstg[s]], [Bwg[s]])
            cp(kb, POOL, wub[s], stu[s], [Bstu[s]], [Bwu[s]])
            for half in range(2):
                g = cnt["gu"] % 3; cnt["gu"] += 1
                pg, pu = C.pb[2 * g], C.pb[2 * g + 1]
                Bpg, Bpu = C.Bpb[2 * g], C.Bpb[2 * g + 1]
                c0 = half * 512
                hsubs = BhT[half * 4:(half + 1) * 4]
                for dc in range(8):
                    mm(kb, pg, wgb[s][:, dc, :], hT[:, dc, c0:c0 + 512], dc == 0, dc == 7, [Bwg[s]] + hsubs, [Bpg])
                for dc in range(8):
                    mm(kb, pu, wub[s][:, dc, :], hT[:, dc, c0:c0 + 512], dc == 0, dc == 7, [Bwu[s]] + hsubs, [Bpu])
                q = cnt["sg"] % 2; cnt["sg"] += 1
                act(kb, sg[q], pg, AF.Silu, [Bpg], [Bsg[q]])
                tt(kb, DVE, actt[:, fc, c0:c0 + 512], sg[q], pu, ALU.mult, [Bsg[q], Bpu], [Bact[fc]])
                if tab is not None:
                    for _ in range(4):
                        if next(tgen, "end") == "end":
                            break
        for sub in range(8):
            r0 = t0 + sub * 128
            s = cnt["y"] % 2; cnt["y"] += 1
            kb.dma(SP, xr[s], x_in[r0:r0 + 128, :], reads=[Bxin], writes=[Bxr[s]])
            g = cnt["py"] % 3; cnt["py"] += 1
            pys = [C.pb[2 * g], C.pb[2 * g + 1]]
            Bpys = [C.Bpb[2 * g], C.Bpb[2 * g + 1]]
            for half in range(2):
                for fc in range(NFC):
                    mm(kb, pys[half], actt[:, fc, sub * 128:(sub + 1) * 128], wdb[:, fc, half * 512:(half + 1) * 512],
                       fc == 0, fc == NFC - 1, [Bact[fc], Bwd[fc]], [Bpys[half]])
            emit_postnorm_residual(kb, C, pys, Bpys, ys[s], Bys[s], xr[s], Bxr[s], st[s], Bst[s], gpost, Bgpost, 4.0 / D, C.eps[:, 1:2])
            kb.dma(SP, x_out[r0:r0 + 128, :], xr[s], reads=[Bxr[s]], writes=[Bxout])
    if tab is not None:
        for _ in tgen:
            pass
    kb.barrier()


def emit_postnorm_residual(kb, C, pys, Bpys, ys, Bys, xr, Bxr, st, Bst, gpost, Bgpost, scale, bias_ap):
    for half in range(2):
        act(kb, ys[:, half * 512:(half + 1) * 512], pys[half], AF.Square, [Bpys[half]], [Bys, Bst],
            accum=st[:, 5 + half:6 + half])
    tt(kb, DVE, st[:, 7:8], st[:, 5:6], st[:, 6:7], ALU.add, [Bst], [Bst])
    rms_stats(kb, st, Bst, 7, 2, 1, scale, bias_ap, C.Bconst)
    for half in range(2):
        stt(kb, ys[:, half * 512:(half + 1) * 512], pys[half], st[:, 2:3], gpost[:, half * 512:(half + 1) * 512],
            ALU.mult, ALU.mult, [Bpys[half], Bst, Bgpost], [Bys])
    tt(kb, POOL, xr, xr, ys, ALU.add, [Bys, Bxr], [Bxr])


INCOLS = 5632
NPAR = 1080
PC_LR, PC_LI, PC_LDT, PC_D, PC_CW, PC_BR, PC_BI, PC_CR, PC_CI = 0, 16, 32, 48, 50, 56, 312, 568, 824
I32 = mybir.dt.int32
SEQ = 2048


class WChunk:
    def __init__(self, kb, A, w_in, nslot=2):
        self.kb = kb
        self.wv = w_in.rearrange("(dc dp) f -> dp dc f", dp=128)
        self.st = [A.alloc([128, 8, 128], F32) for _ in range(2)]; self.Bst = bufs("wst", 2)
        self.wb = [A.alloc([128, 8, 128], BF16) for _ in range(nslot)]; self.Bwb = bufs("wbf", nslot)
        self.n = 0
        self.nslot = nslot

    def load(self, col0):
        s = self.n % 2
        b = self.n % self.nslot
        self.n += 1
        self.kb.dma(SP, self.st[s], self.wv[:, :, col0:col0 + 128], writes=[self.Bst[s]])
        cp(self.kb, POOL, self.wb[b], self.st[s], [self.Bst[s]], [self.Bwb[b]])
        return self.wb[b], self.Bwb[b]


def emit_phase_incs(kb, P, dl, R, Wr):
    ts(kb, DVE, P(9), P(8), 0.0, ALU.is_lt, R, Wr, s2=2 * PI, op1=ALU.mult)
    tt(kb, DVE, dl[:, 0, :], P(8), P(9), ALU.add, R, Wr)
    for k in range(10):
        ts(kb, DVE, P(20), dl[:, k, :], 2.0, ALU.mult, R, Wr)
        ts(kb, DVE, P(21), P(20), 2 * PI, ALU.is_ge, R, Wr)
        stt(kb, dl[:, k + 1, :], P(21), -2 * PI, P(20), ALU.mult, ALU.add, R, Wr)


def table_steps(kb, C, col, dl, Bprm, cT, sT, Bt, ta, tb, tab_tmp, Btt, tabscr, Btsc):
    RT = [Bt, Btt, Bprm]
    mset(kb, DVE, sT[:, 0:1], 0.0, [Bt])
    yield
    for k in range(11):
        n = 1 << k
        dk = dl[:, k, col:col + 1]
        ts(kb, DVE, ta[:, 0:n], sT[:, 0:n], dk, ALU.add, RT, [Btt])
        yield
        ts(kb, DVE, tb[:, 0:n], ta[:, 0:n], 2 * PI, ALU.is_ge, RT, [Btt])
        yield
        stt(kb, sT[:, n:2 * n], tb[:, 0:n], -2 * PI, ta[:, 0:n], ALU.mult, ALU.add, RT, [Bt])
        yield
    ts(kb, DVE, tab_tmp, sT, 1.5 * PI, ALU.is_gt, RT, [Btt], s2=2 * PI, op1=ALU.mult)
    yield
    stt(kb, cT, sT, -PI / 2, tab_tmp, ALU.add, ALU.subtract, RT, [Bt])
    yield
    act(kb, cT, cT, AF.Sin, [Bt], [Bt])
    act(kb, sT, sT, AF.Sin, [Bt, C.Bconst], [Bt], bias=C.eps[:, 2:3])
    kb.dma(SP, tabscr[col, 0], cT, reads=[Bt], writes=[Btsc])
    kb.dma(SP, tabscr[col, 1], sT, reads=[Bt], writes=[Btsc])
    yield


def emit_table(*a):
    for _ in table_steps(*a):
        pass


def emit_mixer_params(kb, C, par, Bpar, prm, Bprm, pw, dl, BT, CT, Bbc, tmpA, tmpB, Btmp, Bx, BBx, tables_only=False):
    def P(i):
        return prm[:, i, :]
    R, Wr = [Bprm, Bpar], [Bprm]
    act(kb, P(0), par[:, PC_LDT:PC_LDT + 16], AF.Exp, R, Wr)
    ts(kb, DVE, P(1), par[:, PC_LR:PC_LR + 16], -1e-4, ALU.min, R, Wr)
    tt(kb, DVE, P(2), P(1), P(0), ALU.mult, R, Wr)
    act(kb, P(3), P(2), AF.Exp, R, Wr)
    tt(kb, DVE, P(4), par[:, PC_LI:PC_LI + 16], P(0), ALU.mult, R, Wr)
    ts(kb, DVE, P(5), P(4), 1.0 / (2 * PI), ALU.mult, R, Wr)
    cp(kb, DVE, P(6).bitcast(I32), P(5), R, Wr)
    cp(kb, DVE, P(7), P(6).bitcast(I32), R, Wr)
    stt(kb, P(8), P(7), -2 * PI, P(4), ALU.mult, ALU.add, R, Wr)
    ts(kb, DVE, P(9), P(8), PI, ALU.is_gt, R, Wr, s2=2 * PI, op1=ALU.mult)
    tt(kb, DVE, P(8), P(8), P(9), ALU.subtract, R, Wr)
    ts(kb, DVE, P(9), P(8), -PI, ALU.is_lt, R, Wr, s2=2 * PI, op1=ALU.mult)
    tt(kb, DVE, P(8), P(8), P(9), ALU.add, R, Wr)
    act(kb, pw[:, 0, 1, :], P(8), AF.Sin, R, Wr)
    ts(kb, DVE, P(11), P(8), PI / 2, ALU.add, R, Wr)
    ts(kb, DVE, P(9), P(11), PI, ALU.is_gt, R, Wr, s2=2 * PI, op1=ALU.mult)
    tt(kb, DVE, P(11), P(11), P(9), ALU.subtract, R, Wr)
    act(kb, pw[:, 0, 0, :], P(11), AF.Sin, R, Wr)
    if tables_only:
        emit_phase_incs(kb, P, dl, R, Wr)
        return
    tt(kb, DVE, P(13), P(3), pw[:, 0, 0, :], ALU.mult, R, Wr)
    tt(kb, DVE, P(14), P(3), pw[:, 0, 1, :], ALU.mult, R, Wr)
    tt(kb, DVE, P(15), P(1), P(1), ALU.mult, R, Wr)
    tt(kb, DVE, P(16), par[:, PC_LI:PC_LI + 16], par[:, PC_LI:PC_LI + 16], ALU.mult, R, Wr)
    tt(kb, DVE, P(15), P(15), P(16), ALU.add, R, Wr)
    recip(kb, P(15), P(15), R, Wr)
    ts(kb, DVE, P(16), P(13), -1.0, ALU.add, R, Wr)
    tt(kb, DVE, P(17), P(16), P(1), ALU.mult, R, Wr)
    tt(kb, DVE, P(18), P(14), par[:, PC_LI:PC_LI + 16], ALU.mult, R, Wr)
    tt(kb, DVE, P(17), P(17), P(18), ALU.add, R, Wr)
    tt(kb, DVE, P(17), P(17), P(15), ALU.mult, R, Wr)
    tt(kb, DVE, P(18), P(14), P(1), ALU.mult, R, Wr)
    tt(kb, DVE, P(19), P(16), par[:, PC_LI:PC_LI + 16], ALU.mult, R, Wr)
    tt(kb, DVE, P(18), P(18), P(19), ALU.subtract, R, Wr)
    tt(kb, DVE, P(18), P(18), P(15), ALU.mult, R, Wr)
    emit_phase_incs(kb, P, dl, R, Wr)
    fre = P(17).unsqueeze(2).broadcast_to([128, 16, 16])
    fim = P(18).unsqueeze(2).broadcast_to([128, 16, 16])
    br = par[:, PC_BR:PC_BR + 256].rearrange("p (c h) -> p c h", c=16)
    bi = par[:, PC_BI:PC_BI + 256].rearrange("p (c h) -> p c h", c=16)
    tA = tmpA[:, 0:256].rearrange("p (c h) -> p c h", c=16)
    tB = tmpB[:, 0:256].rearrange("p (c h) -> p c h", c=16)
    tC = tmpA[:, 256:512].rearrange("p (c h) -> p c h", c=16)
    tD = tmpB[:, 256:512].rearrange("p (c h) -> p c h", c=16)
    RT = [Bprm, Bpar, Btmp]
    tt(kb, DVE, tA, br, fre, ALU.mult, RT, [Btmp])
    tt(kb, DVE, tB, bi, fim, ALU.mult, RT, [Btmp])
    tt(kb, DVE, tA, tA, tB, ALU.subtract, RT, [Btmp])
    tt(kb, DVE, tC, bi, fre, ALU.mult, RT, [Btmp])
    tt(kb, DVE, tD, br, fim, ALU.mult, RT, [Btmp])
    tt(kb, DVE, tC, tC, tD, ALU.add, RT, [Btmp])
    n = 0
    for ri, src in enumerate([tA, tC]):
        for col in range(16):
            po = 32 * (col % 4)
            s = n % 2; n += 1
            mset(kb, DVE, Bx[s], 0.0, [BBx[s]])
            cp(kb, DVE, Bx[s][0:64, po:po + 16], src[0:64, col, :], [Btmp], [BBx[s]])
            cp(kb, DVE, Bx[s][64:128, po + 16:po + 32], src[64:128, col, :], [Btmp], [BBx[s]])
            pbk = 4 + s
            mm(kb, C.pb[pbk][:, 0:128], Bx[s], C.identb, True, True, [BBx[s], C.Bconst], [C.Bpb[pbk]])
            act(kb, BT[:, ri * 16 + col, :], C.pb[pbk][:, 0:128], AF.Copy, [C.Bpb[pbk]], [Bbc])
    mset(kb, DVE, CT, 0.0, [Bbc])
    cr = par[:, PC_CR:PC_CR + 256].rearrange("p (c h) -> p c h", c=16)
    ci = par[:, PC_CI:PC_CI + 256].rearrange("p (c h) -> p c h", c=16)
    for kind, (src, sgn) in enumerate([(cr, 1.0), (cr, -1.0), (ci, -1.0)]):
        for col in range(16):
            po = 32 * (col % 4)
            ts(kb, DVE, CT[0:64, kind * 16 + col, po:po + 16], src[0:64, col, :], sgn, ALU.mult, [Bpar], [Bbc])
            ts(kb, DVE, CT[64:128, kind * 16 + col, po + 16:po + 32], src[64:128, col, :], sgn, ALU.mult, [Bpar], [Bbc])


def emit_mixer(kb, C, x_in, x_out, W, Bxin, Bxout, nseq=2, dbg=None, tables_ready=False):
    C.new_phase()
    A = C.A
    par = A.alloc([128, NPAR], F32); Bpar = Buf("par")
    prm = A.alloc([128, 24, 16], F32); Bprm = Buf("prm")
    pw = A.alloc([128, 1, 2, 16], F32)
    dl = A.alloc([128, 11, 16], F32)
    Btsc = bufs('tabscr', 16)
    BT = A.alloc([128, 32, 128], BF16)
    CT = A.alloc([128, 48, 128], BF16); Bbc = Buf("bc")
    hT = A.alloc([128, 8, SEQ], BF16); BhT = bufs("hT", 16)
    ca = A.alloc([128, 2, SEQ], BF16); Bca = bufs("ca", 2)
    z = A.alloc([128, 2, SEQ], BF16); Bz = bufs("z", 2)
    st = [A.alloc([128, 8], F32) for _ in range(2)]; Bst = bufs("st", 2)
    uo_off = A.off
    u32 = A.alloc([128, 2, SEQ], F32)
    ubf = A.alloc([128, 2, SEQ], BF16)
    A.off = uo_off
    oT = A.alloc([128, 4, SEQ], BF16)
    A.off = max(A.off, uo_off + (2 * SEQ * 4 + 2 * SEQ * 2) // 4)
    tmpA = A.alloc([128, 512], F32); tmpB = A.alloc([128, 512], F32); Btmp = Buf("ptmp")
    Bx = [A.alloc([128, 128], BF16) for _ in range(2)]; BBx = bufs("Bx", 2)
    local0 = A.off

    kb.dma(SP, par, W["par"], writes=[Bpar])

    cnt = {"x": 0, "tp": 0, "pb": 0, "tab": 0, "m": 0, "S": 0, "o": 0, "g": 0, "y": 0}

    def nextpb(n=6, base=0):
        i = base + cnt["pb"] % n
        cnt["pb"] += 1
        return C.pb[i], C.Bpb[i]

    for sq in range(nseq):
        tok0 = sq * SEQ
        Bu = bufs("u", 2)
        BoT = Buf("oT")
        if sq > 0:
            C.new_phase()
        A.off = local0
        gpre = A.alloc([128, D], F32); Bgpre = Buf("gpre")
        kb.dma(SP, gpre, W["gpre"].partition_broadcast(128), writes=[Bgpre])
        xs = [A.alloc([128, D], F32) for _ in range(2)]; Bxs = bufs("xs", 2)
        hb = [A.alloc([128, D], BF16) for _ in range(2)]; Bhb = bufs("hb", 2)
        wc = WChunk(kb, A, W["w_in"])
        ab = A.alloc([128, SEQ], F32); Bab = Buf("ab")
        ac = A.alloc([128, SEQ], F32); Bac = Buf("ac")
        cc = A.alloc([128, SEQ], F32); Bcc = Buf("cc")
        emit_norm_transpose(kb, C, x_in, Bxin, tok0, 16, gpre, Bgpre, hT, BhT, xs, Bxs, hb, Bhb, st, Bst, cnt)
        if sq == 0:
            emit_mixer_params(kb, C, par, Bpar, prm, Bprm, pw, dl, BT, CT, Bbc, tmpA, tmpB, Btmp, Bx, BBx)

        def proj(wt, Bwt, tile):
            ps, Bps = nextpb()
            hs = BhT[tile * 4:(tile + 1) * 4]
            for dc in range(8):
                mm(kb, ps, wt[:, dc, :], hT[:, dc, tile * 512:(tile + 1) * 512], dc == 0, dc == 7, [Bwt] + hs, [Bps])
            return ps, Bps

        for cch in range(2):
            wt, Bwt = wc.load(0 + 128 * cch)
            for tile in range(4):
                ps, Bps = proj(wt, Bwt, tile)
                act(kb, ab[:, tile * 512:(tile + 1) * 512], ps, AF.Copy, [Bps], [Bab])
            wt, Bwt = wc.load(256 + 128 * cch)
            for tile in range(4):
                ps, Bps = proj(wt, Bwt, tile)
                act(kb, ac[:, tile * 512:(tile + 1) * 512], ps, AF.Copy, [Bps], [Bac])
            wt, Bwt = wc.load(512 + 128 * cch)
            for tile in range(4):
                ps, Bps = proj(wt, Bwt, tile)
                sl = slice(tile * 512, (tile + 1) * 512)
                tt(kb, DVE, ac[:, sl], ac[:, sl], ps, ALU.mult, [Bac, Bps], [Bac])
            cw = PC_CW + 3 * cch
            ts(kb, DVE, cc, ac, par[:, cw + 1:cw + 2], ALU.mult, [Bac, Bpar], [Bcc])
            stt(kb, cc[:, 1:SEQ], ac[:, 0:SEQ - 1], par[:, cw:cw + 1], cc[:, 1:SEQ], ALU.mult, ALU.add, [Bac, Bpar, Bcc], [Bcc])
            stt(kb, cc[:, 0:SEQ - 1], ac[:, 1:SEQ], par[:, cw + 2:cw + 3], cc[:, 0:SEQ - 1], ALU.mult, ALU.add, [Bac, Bpar, Bcc], [Bcc])
            tt(kb, DVE, ca[:, cch, :], ab, cc, ALU.mult, [Bab, Bcc], [Bca[cch]])
        for uc in range(2):
            wt, Bwt = wc.load(768 + 128 * uc)
            for tile in range(4):
                ps, Bps = proj(wt, Bwt, tile)
                sl = slice(tile * 512, (tile + 1) * 512)
                act(kb, u32[:, uc, sl], ps, AF.Copy, [Bps], [Bu[uc]])
                cp(kb, DVE, ubf[:, uc, sl], ps, [Bps], [Bu[uc]])
        if dbg is not None and sq == 0:
            kb.dma(SP, dbg["ca"].rearrange("c p t -> p c t"), ca, reads=Bca, writes=[Buf("dbgo")])
            kb.dma(SP, dbg["u"].rearrange("c p t -> p c t"), u32, reads=Bu, writes=[Buf("dbgo2")])
        kb.barrier()

        C.new_phase(); A.off = local0
        cosT = [A.alloc([128, SEQ], F32) for _ in range(2)]
        sinT = [A.alloc([128, SEQ], F32) for _ in range(2)]; Btab = bufs("tab", 2)
        Btt = Buf("ttmp")
        if not tables_ready:
            tab_tmp = A.alloc([128, SEQ], F32); ta = tab_tmp[:, 0:1024]; tb = tab_tmp[:, 1024:2048]
        nsst = 2 if tables_ready else 1
        mtmp = [A.alloc([128, 2, 512], F32)] * 2; Bm = [Buf("mtmp")] * 2; Bm0 = Buf("mt0"); Bm1 = Buf("mt1")
        btr = A.alloc([128, SEQ], F32); bti = A.alloc([128, SEQ], F32); Bbt = bufs("bt", 2)
        sst_bufs = [(A.alloc([128, SEQ], F32), A.alloc([128, SEQ], F32), bufs("sst", 2)) for _ in range(nsst)]
        prd = [A.alloc([128, 4, 512], BF16) for _ in range(2)]; Bprd = bufs("prd", 2)
        yt = [A.alloc([128, 512], F32) for _ in range(2)]; Byt = bufs("yt", 2)
        for uc in range(2):
            first = True
            for ptl in range(4):
                pt = 4 * uc + ptl
                for dr in range(2):
                    col = dr * 8 + pt
                    tb_i = cnt["tab"] % 2; cnt["tab"] += 1
                    cT, sT, Bt = cosT[tb_i], sinT[tb_i], Btab[tb_i]
                    if sq == 0 and not tables_ready:
                        emit_table(kb, C, col, dl, Bprm, cT, sT, Bt, ta, tb, tab_tmp, Btt, W["tabscr"], Btsc[col])
                    else:
                        kb.dma(SP, cT, W["tabscr"][col, 0], reads=[Btsc[col]], writes=[Bt])
                        kb.dma(SP, sT, W["tabscr"][col, 1], reads=[Btsc[col]], writes=[Bt])

                    def tview(tab, tile):
                        if dr == 0:
                            return tab[:, tile * 512:(tile + 1) * 512]
                        lo = SEQ - (tile + 1) * 512
                        return tab[:, lo:lo + 512][:, ::-1]
                    for tile in range(4):
                        sl = slice(tile * 512, (tile + 1) * 512)
                        pr, Bpr = C.pb[4], C.Bpb[4]
                        pi_, Bpi = C.pb[5], C.Bpb[5]
                        mm(kb, pr, BT[:, col, :], ubf[:, uc, sl], True, True, [Bbc, Bu[uc]], [Bpr])
                        mm(kb, pi_, BT[:, 16 + col, :], ubf[:, uc, sl], True, True, [Bbc, Bu[uc]], [Bpi])
                        m = cnt["m"] % 2; cnt["m"] += 1
                        mt = mtmp[m]
                        tt(kb, DVE, btr[:, sl], pr, tview(cT, tile), ALU.mult, [Bpr, Bt], [Bbt[0]])
                        tt(kb, DVE, mt[:, 0, :], pi_, tview(sT, tile), ALU.mult, [Bpi, Bt], [Bm0])
                        tt(kb, DVE, bti[:, sl], pi_, tview(cT, tile), ALU.mult, [Bpi, Bt], [Bbt[1]])
                        tt(kb, DVE, mt[:, 1, :], pr, tview(sT, tile), ALU.mult, [Bpr, Bt], [Bm1])
                        tt(kb, DVE, btr[:, sl], btr[:, sl], mt[:, 0, :], ALU.add, [Bm0, Bbt[0]], [Bbt[0]])
                        tt(kb, DVE, bti[:, sl], bti[:, sl], mt[:, 1, :], ALU.subtract, [Bm1, Bbt[1]], [Bbt[1]])
                    str_, sti, Bsst = sst_bufs[cnt["tab"] % nsst]
                    dec = prm[:, 3, col:col + 1].to_broadcast([128, SEQ])
                    if dr == 0:
                        kb.op(DVE, (lambda o=str_, d=dec, i=btr: lambda e: e.tensor_tensor_scan(o, d, i, 0.0, ALU.mult, ALU.add))(),
                              [Bbt[0], Bprm], [Bsst[0]])
                        kb.op(DVE, (lambda o=sti, d=dec, i=bti: lambda e: e.tensor_tensor_scan(o, d, i, 0.0, ALU.mult, ALU.add))(),
                              [Bbt[1], Bprm], [Bsst[1]])
                    else:
                        kb.op(DVE, (lambda o=str_[:, ::-1], d=dec, i=btr[:, ::-1]: lambda e: e.tensor_tensor_scan(o, d, i, 0.0, ALU.mult, ALU.add))(),
                              [Bbt[0], Bprm], [Bsst[0]])
                        kb.op(DVE, (lambda o=sti[:, ::-1], d=dec, i=bti[:, ::-1]: lambda e: e.tensor_tensor_scan(o, d, i, 0.0, ALU.mult, ALU.add))(),
                              [Bbt[1], Bprm], [Bsst[1]])
                    last = (ptl == 3 and dr == 1)
                    for tile in range(4):
                        sl = slice(tile * 512, (tile + 1) * 512)
                        m = cnt["m"] % 2; cnt["m"] += 1
                        pd = prd[m]
                        tt(kb, POOL, pd[:, 0, :], tview(cT, tile), str_[:, sl], ALU.mult, [Bt, Bsst[0]], [Bprd[m]])
                        tt(kb, POOL, pd[:, 1, :], tview(sT, tile), sti[:, sl], ALU.mult, [Bt, Bsst[1]], [Bprd[m]])
                        tt(kb, POOL, pd[:, 2, :], tview(sT, tile), str_[:, sl], ALU.mult, [Bt, Bsst[0]], [Bprd[m]])
                        tt(kb, POOL, pd[:, 3, :], tview(cT, tile), sti[:, sl], ALU.mult, [Bt, Bsst[1]], [Bprd[m]])
                        py, Bpy = C.pb[tile], C.Bpb[tile]
                        kinds = [0, 1, 2, 2]
                        for j in range(4):
                            mm(kb, py, CT[:, kinds[j] * 16 + col, :], pd[:, j, :], first and j == 0, last and j == 3,
                               [Bbc, Bprd[m]], [Bpy])
                    first = False
            for tile in range(4):
                sl = slice(tile * 512, (tile + 1) * 512)
                q = cnt["y"] % 2; cnt["y"] += 1
                stt(kb, yt[q], u32[:, uc, sl], par[:, PC_D + uc:PC_D + uc + 1], C.pb[tile], ALU.mult, ALU.add,
                    [Bu[uc], Bpar, C.Bpb[tile]], [Byt[q]])
                act(kb, z[:, uc, sl], yt[q], AF.Gelu, [Byt[q]], [Bz[uc]])
                if dbg is not None and sq == 0:
                    kb.dma(SP, dbg["y"][uc, :, sl], yt[q], reads=[Byt[q]], writes=[Buf("dbgy")])
        kb.barrier()

        C.new_phase(); A.off = local0
        wc = WChunk(kb, A, W["w_in"])
        Eb = [A.alloc([128, 640], BF16) for _ in range(2)]; BEb = bufs("Eb", 2)
        qz = [[A.alloc([128, SEQ], BF16) for _ in range(2)] for _ in range(2)]; Bqb = bufs("qb", 2)
        for i in range(2):
            for j in range(2):
                mset(kb, POOL, qz[i][j], 0.0, [Bqb[i]])
            mset(kb, POOL, Eb[i][64:128, 512:640], 0.0, [BEb[i]])
        kbf = [A.alloc([128, SEQ], BF16) for _ in range(2)]; Bkb = bufs("kb", 2)
        Vt = [A.alloc([128, 16, 2, 65], BF16) for _ in range(2)]; BVt = bufs("Vt", 2)
        gst = [A.alloc([128, 1024], F32) for _ in range(2)]; Bgst = bufs("gst", 2)
        Gt = [A.alloc([128, 2, 1024], BF16) for _ in range(2)]; BGt = bufs("Gt", 2)
        otok = A.alloc([128, 16, 512], BF16); Bot = bufs("otok", 16)
        rc = [A.alloc([128, 2], F32) for _ in range(2)]; Brc = bufs("rc", 2)
        for i in range(2):
            mset(kb, DVE, Vt[i][:, :, :, 64:65], 1.0, [BVt[i]])
        for hp in range(4):
            s = hp % 2
            wt, Bwt = wc.load(1024 + 128 * hp)
            for tile in range(4):
                ps, Bps = proj(wt, Bwt, tile)
                act(kb, qz[s][0][0:64, tile * 512:(tile + 1) * 512], ps[0:64, :], AF.Copy, [Bps], [Bqb[s]], scale=0.125)
                act(kb, qz[s][1][64:128, tile * 512:(tile + 1) * 512], ps[64:128, :], AF.Copy, [Bps], [Bqb[s]], scale=0.125)
            wt, Bwt = wc.load(1536 + 128 * hp)
            for tile in range(4):
                ps, Bps = proj(wt, Bwt, tile)
                cp(kb, DVE, kbf[s][:, tile * 512:(tile + 1) * 512], ps, [Bps], [Bkb[s]])
            wt, Bwt = wc.load(2048 + 128 * hp)
            for j in range(4):
                ps, Bps = nextpb()
                for q in range(4):
                    sub = 4 * j + q
                    for dc in range(8):
                        mm(kb, ps[:, q * 128:(q + 1) * 128], hT[:, dc, sub * 128:(sub + 1) * 128], wt[:, dc, :], dc == 0, dc == 7,
                           [Bwt, BhT[sub]], [Bps])
                psv = ps.rearrange("p (a b) -> p a b", a=4)
                act(kb, Vt[s][:, 4 * j:4 * j + 4, 0, 0:64], psv[:, :, 0:64], AF.Copy, [Bps], [BVt[s]])
                cp(kb, DVE, Vt[s][:, 4 * j:4 * j + 4, 1, 0:64], psv[:, :, 64:128], [Bps], [BVt[s]])
            for e_ in range(2):
                h = 2 * hp + e_
                pbase = 64 * e_
                g = cnt["g"] % 2; cnt["g"] += 1
                kb.dma(SP, gst[g], W["gt"][h], writes=[Bgst[g]])
                tt(kb, POOL, Gt[g][:, 0, :], gst[g], C.mk[:, 0, :], ALU.add, [Bgst[g], C.Bconst], [BGt[g]])
                tt(kb, POOL, Gt[g][:, 1, :], gst[g], C.mk[:, 1, :], ALU.add, [Bgst[g], C.Bconst], [BGt[g]])
                def na_front(blk):
                    r0 = 2 * blk
                    if blk < 2:
                        rs0, nrow, gv = 0, 8, 0
                    elif blk >= 14:
                        rs0, nrow, gv = 24, 8, 0
                    else:
                        rs0, nrow, gv = r0 - 4, 9, 1
                    nch = (nrow + 1) // 2
                    si = cnt["S"] % 2; cnt["S"] += 1
                    S = C.pp[si]
                    BS = [C.Bpb[2 * si], C.Bpb[2 * si + 1]]
                    for c in range(nch):
                        kn = 128 if 2 * c + 1 < nrow else 64
                        ksub = rs0 // 2 + c
                        base = rs0 + 2 * c - r0 + 7
                        e0 = 15 - base
                        assert 0 <= e0 <= 14
                        outp = S[0:kn, c * 128:(c + 1) * 128]
                        Bo = [BS[c // 4]]
                        mm(kb, outp, kbf[s][:, ksub * 128:ksub * 128 + kn], qz[s][e_][:, blk * 128:(blk + 1) * 128],
                           True, False, [Bkb[s], Bqb[s]], Bo)
                        mm(kb, outp, C.identb[:, 0:kn], Gt[g][:, gv, e0 * 64:e0 * 64 + 128], False, True,
                           [C.Bconst, BGt[g]], Bo)
                    Ebv = Eb[si]
                    nfull = nch if nrow % 2 == 0 else nch - 1
                    act(kb, Ebv[:, 0:nfull * 128], S[:, 0:nfull * 128], AF.Exp, [BS[0]], [BEb[si]])
                    if nfull < nch:
                        act(kb, Ebv[0:64, 512:640], S[0:64, 512:640], AF.Exp, [BS[1]], [BEb[si]])
                    return (blk, rs0, nrow, nch, si)

                def na_back(job):
                    blk, rs0, nrow, nch, si = job
                    Ebv = Eb[si]
                    oi = 4 + cnt["o"] % 2; cnt["o"] += 1
                    po, Bpo = C.pb[oi], C.Bpb[oi]
                    for c in range(nch):
                        ksub = rs0 // 2 + c
                        mm(kb, po[:, 0:65], Ebv[:, c * 128:(c + 1) * 128], Vt[s][:, ksub, e_, :], c == 0, c == nch - 1,
                           [BEb[si], BVt[s]], [Bpo])
                    ri = oi - 4
                    recip(kb, rc[ri][:, 0:1], po[:, 64:65], [Bpo], [Brc[ri]])
                    ts(kb, DVE, otok[:, blk, h * 64:(h + 1) * 64], po[:, 0:64], rc[ri][:, 0:1], ALU.mult, [Bpo, Brc[ri]], [Bot[blk]])

                prev = None
                for blk in range(16):
                    job = na_front(blk)
                    if prev is not None:
                        na_back(prev)
                    prev = job
                na_back(prev)
        for blk in range(16):
            p = cnt["tp"] % 2; cnt["tp"] += 1
            for q in range(4):
                tpose(kb, C.tp[p][:, q * 128:(q + 1) * 128], otok[:, blk, q * 128:(q + 1) * 128], C.identb,
                      [Bot[blk], C.Bconst], [C.Btp[p]])
            src = C.tp[p][:, 0:512].rearrange("p (a b) -> p a b", a=4)
            dst = oT[:, :, blk * 128:(blk + 1) * 128]
            if blk % 2 == 0:
                act(kb, dst, src, AF.Copy, [C.Btp[p]], [BoT])
            else:
                cp(kb, DVE, dst, src, [C.Btp[p]], [BoT])
        if dbg is not None and sq == 0:
            kb.dma(SP, dbg["o"].rearrange("(b p) c -> p b c", p=128), otok, reads=Bot, writes=[Buf("dbgo3")])
        kb.barrier()

        C.new_phase(); A.off = local0
        mixT = A.alloc([128, 8, SEQ], BF16); Bmix = bufs("mix", 8)
        merge_a0 = A.off
        wc = WChunk(kb, A, W["w_in"], nslot=6)
        wst = [A.alloc([128, D], F32) for _ in range(2)]; Bwst = bufs("wst2", 2)
        woa = A.alloc([128, 2, D], BF16); wga = A.alloc([128, 2, D], BF16); wgb = A.alloc([128, 2, D], BF16)
        woc = A.alloc([128, 4, D], BF16); Bsw = bufs("sw", 4)
        tm = [A.alloc([128, 512], F32) for _ in range(6)]; Btm = bufs("tm", 6)
        nst = 0
        for wi_, (wten, name, nchunk) in enumerate([(woa, "woa", 2), (wga, "wga", 2), (wgb, "wgb", 2), (woc, "woc", 4)]):
            for j in range(nchunk):
                s = nst % 2; nst += 1
                kb.dma(SP, wst[s], W[name][j * 128:(j + 1) * 128, :], writes=[Bwst[s]])
                cp(kb, POOL, wten[:, j, :], wst[s], [Bwst[s]], [Bsw[wi_]])
        for dcp in range(8):
            dsl = slice(dcp * 128, (dcp + 1) * 128)
            wga_, Bwga_ = wc.load(2560 + 128 * dcp)
            wgb_, Bwgb_ = wc.load(3584 + 128 * dcp)
            wgc_, Bwgc_ = wc.load(4608 + 128 * dcp)
            for tile in range(4):
                sl = slice(tile * 512, (tile + 1) * 512)
                k6 = cnt["m"] % 2; cnt["m"] += 1
                t0_, t1_, t2_ = tm[3 * k6], tm[3 * k6 + 1], tm[3 * k6 + 2]
                B0, B1, B2 = Btm[3 * k6], Btm[3 * k6 + 1], Btm[3 * k6 + 2]
                pga, Bpga = proj(wga_, Bwga_, tile)
                pya, Bpya = nextpb()
                for j in range(2):
                    mm(kb, pya, woa[:, j, dsl], ca[:, j, sl], j == 0, j == 1, [Bsw[0], Bca[j]], [Bpya])
                act(kb, t0_, pga, AF.Sigmoid, [Bpga], [B0])
                tt(kb, DVE, t1_, t0_, pya, ALU.mult, [B0, Bpya], [B1])
                pgb, Bpgb = proj(wgb_, Bwgb_, tile)
                py1, Bpy1 = nextpb()
                for j in range(2):
                    mm(kb, py1, wga[:, j, dsl], z[:, j, sl], j == 0, j == 1, [Bsw[1], Bz[j]], [Bpy1])
                py2, Bpy2 = nextpb()
                for j in range(2):
                    mm(kb, py2, wgb[:, j, dsl], z[:, j, sl], j == 0, j == 1, [Bsw[2], Bz[j]], [Bpy2])
                act(kb, t0_, py2, AF.Sigmoid, [Bpy2], [B0])
                tt(kb, DVE, t2_, t0_, py1, ALU.mult, [B0, Bpy1], [B2])
                act(kb, t0_, pgb, AF.Sigmoid, [Bpgb], [B0])
                tt(kb, DVE, t2_, t2_, t0_, ALU.mult, [B0, B2], [B2])
                tt(kb, DVE, t1_, t1_, t2_, ALU.add, [B1, B2], [B1])
                pgc, Bpgc = proj(wgc_, Bwgc_, tile)
                pyc, Bpyc = nextpb()
                for j in range(4):
                    mm(kb, pyc, woc[:, j, dsl], oT[:, j, sl], j == 0, j == 3, [Bsw[3], BoT], [Bpyc])
                act(kb, t0_, pgc, AF.Sigmoid, [Bpgc], [B0])
                tt(kb, DVE, t2_, t0_, pyc, ALU.mult, [B0, Bpyc], [B2])
                tt(kb, DVE, mixT[:, dcp, sl], t1_, t2_, ALU.add, [B1, B2], [Bmix[dcp]])
        kb.barrier()

        C.new_phase(); A.off = merge_a0
        gpost = A.alloc([128, D], F32); Bgpost = Buf("gpost")
        kb.dma(SP, gpost, W["gpost"].partition_broadcast(128), writes=[Bgpost])
        wst = [A.alloc([128, D], F32) for _ in range(2)]; Bwst = bufs("wst3", 2)
        wo = A.alloc([128, 8, D], BF16); Bwo = bufs("wo", 8)
        ys = [A.alloc([128, D], F32) for _ in range(2)]; Bys = bufs("ys", 2)
        xr = [A.alloc([128, D], F32) for _ in range(2)]; Bxr = bufs("xr", 2)
        for j in range(8):
            s = j % 2
            kb.dma(SP, wst[s], W["wo"][j * 128:(j + 1) * 128, :], writes=[Bwst[s]])
            cp(kb, POOL, wo[:, j, :], wst[s], [Bwst[s]], [Bwo[j]])
        for sub in range(16):
            r0 = tok0 + sub * 128
            s = sub % 2
            kb.dma(SP, xr[s], x_in[r0:r0 + 128, :], reads=[Bxin], writes=[Bxr[s]])
            g3 = cnt["y"] % 3; cnt["y"] += 1
            pys = [C.pb[2 * g3], C.pb[2 * g3 + 1]]
            Bpys = [C.Bpb[2 * g3], C.Bpb[2 * g3 + 1]]
            for half in range(2):
                for dc in range(8):
                    mm(kb, pys[half], mixT[:, dc, sub * 128:(sub + 1) * 128], wo[:, dc, half * 512:(half + 1) * 512],
                       dc == 0, dc == 7, [Bmix[dc], Bwo[dc]], [Bpys[half]])
            emit_postnorm_residual(kb, C, pys, Bpys, ys[s], Bys[s], xr[s], Bxr[s], st[s], Bst[s], gpost, Bgpost, 1.0 / D, C.eps[:, 0:1])
            kb.dma(SP, x_out[r0:r0 + 128, :], xr[s], reads=[Bxr[s]], writes=[Bxout])
        kb.barrier()


def host_masks():
    jl = np.arange(2)[:, None, None, None]
    kc = np.arange(64)[None, :, None, None]
    e = np.arange(16)[None, None, :, None]
    qc = np.arange(64)[None, None, None, :]
    dr = 15 - e + jl
    cs = np.clip(qc - 8, 0, 48)
    colok = (kc >= cs) & (kc < cs + 16)
    ok_edge = colok & (dr >= 0) & (dr <= 14)
    ok_int = colok & (dr >= 3) & (dr <= 10)
    m = np.stack([np.where(ok_edge, 0.0, -30000.0), np.where(ok_int, 0.0, -30000.0)]).astype(np.float32)
    return np.ascontiguousarray(np.broadcast_to(m, (2, 2, 64, 16, 64)).reshape(2, 128, 1024))


def host_gt(rpb):
    jl = np.arange(2)[:, None, None, None]
    kc = np.arange(64)[None, :, None, None]
    e = np.arange(16)[None, None, :, None]
    qc = np.arange(64)[None, None, None, :]
    dr = np.clip(15 - e + jl, 0, 14)
    dc = np.clip(kc - qc + 15, 0, 30)
    dr, dc = np.broadcast_arrays(dr, dc)
    return np.ascontiguousarray(rpb[:, dr, dc].reshape(8, 128, 1024).astype(np.float32))


def host_par(lam_re, lam_im, log_dt, b_re, b_im, c_re, c_im, d_skip, conv_w):
    par = np.zeros((128, NPAR), np.float32)

    def cols(a):
        return a.reshape(2, 8, 2, 64).transpose(2, 3, 0, 1).reshape(128, 16)
    par[:, PC_LR:PC_LR + 16] = cols(lam_re)
    par[:, PC_LI:PC_LI + 16] = cols(lam_im)
    par[:, PC_LDT:PC_LDT + 16] = cols(np.broadcast_to(log_dt[:, :, None], (2, 16, 64)))
    par[:, PC_D:PC_D + 2] = d_skip.reshape(2, 128).T
    par[:, PC_CW:PC_CW + 6] = conv_w.reshape(3, 2, 128).transpose(2, 1, 0).reshape(128, 6)
    def bcols(a):
        return a.reshape(2, 8, 2, 64, 16).transpose(2, 3, 0, 1, 4).reshape(128, 256)
    par[:, PC_BR:PC_BR + 256] = bcols(b_re)
    par[:, PC_BI:PC_BI + 256] = bcols(b_im)
    def ccols(a):
        return a.reshape(2, 8, 2, 16, 64).transpose(2, 4, 0, 1, 3).reshape(128, 256)
    par[:, PC_CR:PC_CR + 256] = ccols(c_re)
    par[:, PC_CI:PC_CI + 256] = ccols(c_im)
    return par


def build_ffn_prog(ntok=NTOK):
    nc = bass.Bass("TRN2", target_bir_lowering=False)
    x = nc.dram_tensor("x", [ntok, D], F32, kind="ExternalInput").ap()
    gpre = nc.dram_tensor("gpre", [D], F32, kind="ExternalInput").ap()
    gpost = nc.dram_tensor("gpost", [D], F32, kind="ExternalInput").ap()
    wg = nc.dram_tensor("wg", [D, DFF], F32, kind="ExternalInput").ap()
    wu = nc.dram_tensor("wu", [D, DFF], F32, kind="ExternalInput").ap()
    wd = nc.dram_tensor("wd", [DFF, D], F32, kind="ExternalInput").ap()
    ident = nc.dram_tensor("ident", [128, 128], F32, kind="ExternalInput").ap()
    y = nc.dram_tensor("y", [ntok, D], F32, kind="ExternalOutput").ap()
    with contextlib.ExitStack() as st:
        kb = KB(nc, st)
        C = Ctx(kb, ident)
        emit_ffn(kb, C, x, y, gpre, gpost, wg, wu, wd, Buf("x_in"), Buf("x_out"), ntok=ntok)
        kb.emit()
    return nc


MIX_W = [("gpre", [D]), ("gpost", [D]), ("w_in", [D, INCOLS]), ("par", [128, NPAR]), ("woa", [256, D]), ("wga", [256, D]),
         ("wgb", [256, D]), ("woc", [512, D]), ("wo", [D, D]), ("gt", [8, 128, 1024])]


def build_mixer_prog(nseq=2, dbg=False):
    nc = bass.Bass("TRN2", target_bir_lowering=False)
    ntok = nseq * SEQ
    x = nc.dram_tensor("x", [ntok, D], F32, kind="ExternalInput").ap()
    W = {}
    for name, shp in MIX_W:
        W[name] = nc.dram_tensor(name, shp, F32, kind="ExternalInput").ap()
    ident = nc.dram_tensor("ident", [128, 128], F32, kind="ExternalInput").ap()
    masks = nc.dram_tensor("masks", [2, 128, 1024], F32, kind="ExternalInput").ap()
    y = nc.dram_tensor("y", [ntok, D], F32, kind="ExternalOutput").ap()
    W["tabscr"] = nc.dram_tensor("tabscr", [16, 2, 128, SEQ], F32, kind="Internal").ap()
    d = None
    if dbg:
        d = {"ca": nc.dram_tensor("dbg_ca", [2, 128, SEQ], BF16, kind="ExternalOutput").ap(),
             "u": nc.dram_tensor("dbg_u", [2, 128, SEQ], F32, kind="ExternalOutput").ap(),
             "y": nc.dram_tensor("dbg_y", [2, 128, SEQ], F32, kind="ExternalOutput").ap(),
             "o": nc.dram_tensor("dbg_o", [SEQ, 512], BF16, kind="ExternalOutput").ap()}
    with contextlib.ExitStack() as st:
        kb = KB(nc, st)
        C = Ctx(kb, ident, masks)
        emit_mixer(kb, C, x, y, W, Buf("x_in"), Buf("x_out"), nseq=nseq, dbg=d)
        kb.emit()
    return nc


FFN_W = [("gpre", [D]), ("gpost", [D]), ("wg", [D, DFF]), ("wu", [D, DFF]), ("wd", [DFF, D])]
DEPTH = 4


def build_fused_prog(nseq=2, depth=DEPTH):
    nc = bass.Bass("TRN2", target_bir_lowering=False)
    ntok = nseq * SEQ
    x = nc.dram_tensor("x", [ntok, D], F32, kind="ExternalInput").ap()
    ident = nc.dram_tensor("ident", [128, 128], F32, kind="ExternalInput").ap()
    masks = nc.dram_tensor("masks", [2, 128, 1024], F32, kind="ExternalInput").ap()
    WL = []
    for L in range(depth):
        d = {}
        for blk, spec in (("f1", FFN_W), ("mx", MIX_W), ("f2", FFN_W)):
            d[blk] = {name: nc.dram_tensor("%s_%s_%d" % (blk, name, L), shp, F32, kind="ExternalInput").ap() for name, shp in spec}
        WL.append(d)
    y = nc.dram_tensor("y", [ntok, D], F32, kind="ExternalOutput").ap()
    scr = [nc.dram_tensor("xscr%d" % i, [ntok, D], F32, kind="Internal").ap() for i in range(2)]
    tabscr = nc.dram_tensor("tabscr", [16, 2, 128, SEQ], F32, kind="Internal").ap()
    for L in range(depth):
        WL[L]["mx"]["tabscr"] = tabscr
    with contextlib.ExitStack() as st:
        kb = KB(nc, st)
        C = Ctx(kb, ident, masks)
        cur, Bcur = x, Buf("x_in")
        nsub = 3 * depth
        k = 0
        for L in range(depth):
            for blk in ("f1", "mx", "f2"):
                k += 1
                dst = y if k == nsub else scr[k % 2]
                Bdst = Buf("xo%d" % k)
                w = WL[L][blk]
                if blk == "mx":
                    emit_mixer(kb, C, cur, dst, w, Bcur, Bdst, nseq=nseq, tables_ready=True)
                else:
                    tab = (WL[L]["mx"]["par"], tabscr) if blk == "f1" else None
                    emit_ffn(kb, C, cur, dst, w["gpre"], w["gpost"], w["wg"], w["wu"], w["wd"], Bcur, Bdst, ntok=ntok, tab=tab)
                cur, Bcur = dst, Bdst
        kb.emit()
    return nc


def _layer_inputs(inp, L):
    g = lambda k: np.ascontiguousarray(np.asarray(inp[k])[L], dtype=np.float32)
    m = {}
    f1 = {"gpre": g("norm_ffn1_pre"), "gpost": g("norm_ffn1_post"), "wg": g("ffn1_w_gate"), "wu": g("ffn1_w_up"),
          "wd": g("ffn1_w_down")}
    f2 = {"gpre": g("norm_ffn2_pre"), "gpost": g("norm_ffn2_post"), "wg": g("ffn2_w_gate"), "wu": g("ffn2_w_up"),
          "wd": g("ffn2_w_down")}
    mx = {"gpre": g("norm_mix_pre"), "gpost": g("norm_mix_post"), "w_in": g("w_in"),
          "par": host_par(g("ssm_lam_re"), g("ssm_lam_im"), g("ssm_log_dt"), g("ssm_b_re"), g("ssm_b_im"),
                          g("ssm_c_re"), g("ssm_c_im"), g("ssm_d"), g("conv_w")),
          "woa": g("w_out_a"), "wga": g("w_glu_a"), "wgb": g("w_glu_b"), "woc": g("w_out_c"), "wo": g("w_o"),
          "gt": host_gt(g("na_rpb"))}
    for blk, d in (("f1", f1), ("mx", mx), ("f2", f2)):
        for name, v in d.items():
            m["%s_%s_%d" % (blk, name, L)] = v
    return m


def kernel(**inp):
    x = np.ascontiguousarray(np.asarray(inp["x"], dtype=np.float32))
    B = x.shape[0]
    per = B // NCORES
    common = {"ident": np.eye(128, dtype=np.float32), "masks": host_masks()}
    for L in range(DEPTH):
        common.update(_layer_inputs(inp, L))
    nc = build_fused_prog(per)
    cores = list(range(NCORES))
    in_maps = [dict(common, x=np.ascontiguousarray(x[c * per:(c + 1) * per].reshape(per * SEQ, D))) for c in cores]
    res = run_bass_kernel_spmd(nc, in_maps, core_ids=cores)
    out = np.stack([np.asarray(res.results[c]["y"]).reshape(per, SEQ, D) for c in cores]).reshape(B, SEQ, D)
    return out.astype(np.float32)
```

```python
import contextlib
import numpy as np
import concourse.bass as bass
import concourse.mybir as mybir
from concourse.bass_utils import run_bass_kernel_spmd

F32 = mybir.dt.float32
BF16 = mybir.dt.bfloat16
AF = mybir.ActivationFunctionType
ALU = mybir.AluOpType

PE, DVE, ACT, POOL, SP = "pe", "dve", "act", "pool", "sp"
ENGS = [PE, DVE, ACT, POOL, SP]

D = 1024
DFF = 2816
NFC = DFF // 128
NTOK = 4096
NCORES = 8
EPS = 1e-6
PI = 3.14159265358979


_UID = [0]


class Buf:
    __slots__ = ("name", "lw", "rd", "sem", "semcnt", "uid")

    def __init__(self, name):
        self.name = name
        self.lw = None
        self.rd = []
        self.sem = None
        self.semcnt = 0
        _UID[0] += 1
        self.uid = _UID[0]


class Op:
    __slots__ = ("eng", "fn", "waits", "signal", "pos", "val")

    def __init__(self, eng, fn):
        self.eng = eng
        self.fn = fn
        self.waits = []
        self.signal = False
        self.pos = 0
        self.val = 0


class KB:
    def __init__(self, nc, stack):
        self.nc = nc
        self.stack = stack
        self.streams = {e: [] for e in ENGS}
        self.waited = {}
        self.waited_dma = {}
        self.free_sems = []
        self.dma_bufs = []
        self.epoch = 0
        self.nsem = 0

    def sbuf(self, name, shape, dt):
        return self.stack.enter_context(self.nc.sbuf_tensor(name, list(shape), dt))

    def psum(self, name, shape, dt):
        return self.stack.enter_context(self.nc.psum_tensor(name, list(shape), dt))

    def new_sem(self, name):
        self.nsem += 1
        return self.stack.enter_context(self.nc.semaphore(name))

    def _deps(self, reads, writes):
        deps = []
        for b in reads:
            if b.lw is not None:
                deps.append(b.lw)
        for b in writes:
            if b.lw is not None:
                deps.append(b.lw)
            deps.extend(b.rd)
        return deps

    def _add_waits(self, op, deps):
        eng = op.eng
        best = {}
        for d in deps:
            if d[0] == "op":
                o = d[1]
                if o.eng == PE and eng == PE:
                    continue
                k = (eng, o.eng)
                if o.pos <= self.waited.get(k, 0):
                    continue
                if k not in best or best[k].pos < o.pos:
                    best[k] = o
            else:
                _, sem, v, uid, ep = d
                if ep < self.epoch:
                    continue
                k = (eng, id(sem))
                if v <= self.waited_dma.get(k, 0):
                    continue
                self.waited_dma[k] = v
                op.waits.append(d)
        for k, o in best.items():
            self.waited[k] = o.pos
            o.signal = True
            op.waits.append(("op", o))

    def op(self, eng, fn, reads=(), writes=(), deps=()):
        o = Op(eng, fn)
        self.streams[eng].append(o)
        o.pos = len(self.streams[eng])
        self._add_waits(o, self._deps(reads, writes) + list(deps))
        d = ("op", o)
        for b in reads:
            b.rd.append(d)
        for b in writes:
            b.lw = d
            b.rd = []
        return d

    def dma(self, eng, out, in_, reads=(), writes=(), deps=()):
        sb = (list(writes) + list(reads))[0]
        if sb.sem is None:
            if self.free_sems:
                sb.sem, sb.semcnt = self.free_sems.pop()
            else:
                sb.sem = self.new_sem("d%d" % self.nsem)
                sb.semcnt = 0
            self.dma_bufs.append(sb)
        sb.semcnt += 16
        val = sb.semcnt
        sem = sb.sem

        def fn(e):
            return e.dma_start(out=out, in_=in_), sem
        o = Op(eng, fn)
        o.signal = "dma"
        self.streams[eng].append(o)
        o.pos = len(self.streams[eng])
        self._add_waits(o, self._deps(reads, writes) + list(deps))
        d = ("dma", sem, val, sb.uid, self.epoch)
        for b in reads:
            b.rd.append(d)
        for b in writes:
            b.lw = d
            b.rd = []
        return d

    def _all_deps(self):
        deps = []
        for e in ENGS:
            for o in reversed(self.streams[e]):
                if o.fn is not None and o.signal != "dma":
                    deps.append(("op", o))
                    break
        for b in self.dma_bufs:
            deps.append(("dma", b.sem, b.semcnt, b.uid, self.epoch))
        return deps

    def barrier(self):
        deps = self._all_deps()
        for e in ENGS:
            o = Op(e, None)
            self.streams[e].append(o)
            o.pos = len(self.streams[e])
            self._add_waits(o, deps)
        for b in self.dma_bufs:
            self.free_sems.append((b.sem, b.semcnt))
            b.sem = None
        self.dma_bufs = []
        self.epoch += 1

    def emit(self):
        nc = self.nc
        esem = {e: self.new_sem("e_" + e) for e in ENGS}
        fin = Op(SP, None)
        self.streams[SP].append(fin)
        fin.pos = len(self.streams[SP])
        self._add_waits(fin, self._all_deps())
        for e in ENGS:
            c = 0
            for o in self.streams[e]:
                if o.signal is True:
                    c += 1
                    o.val = c

        def run(e, engine):
            for o in self.streams[e]:
                for w in o.waits:
                    if w[0] == "op":
                        engine.wait_ge(esem[w[1].eng], w[1].val)
                    else:
                        engine.wait_ge(w[1], w[2])
                if o.fn is None:
                    continue
                if o.signal == "dma":
                    ins, sem = o.fn(engine)
                    ins.then_inc(sem, 16)
                else:
                    ins = o.fn(engine)
                    if o.signal:
                        ins.then_inc(esem[e], 1)

        with nc.Block() as block:
            @block.sync
            def _(eng):
                run(SP, eng)

            @block.scalar
            def _(eng):
                run(ACT, eng)

            @block.vector
            def _(eng):
                run(DVE, eng)

            @block.gpsimd
            def _(eng):
                run(POOL, eng)

            @block.tensor
            def _(eng):
                run(PE, eng)


def bufs(name, n):
    return [Buf("%s%d" % (name, i)) for i in range(n)]


def _prod(xs):
    r = 1
    for v in xs:
        r *= v
    return r


class Arena:
    def __init__(self, kb, nbytes):
        self.cap = nbytes // 4
        self.t = kb.sbuf("arena", [128, self.cap], F32)
        self.off = 0

    def alloc(self, shape, dt):
        n = _prod(shape[1:])
        words = (n * (2 if dt == BF16 else 4) + 3) // 4
        words = (words + 7) // 8 * 8
        assert self.off + words <= self.cap, ("arena overflow", self.off, words, self.cap)
        v = self.t[:, self.off:self.off + words]
        self.off += words
        self.peak = max(getattr(self, "peak", 0), self.off)
        if dt == BF16:
            v = v.bitcast(BF16)
        v = v[0:shape[0], 0:n]
        if len(shape) == 3:
            v = v.rearrange("p (a b) -> p a b", a=shape[1])
        elif len(shape) == 4:
            v = v.rearrange("p (a b c) -> p a b c", a=shape[1], b=shape[2])
        return v


def mm(kb, out, lhsT, rhs, start, stop, reads, writes):
    return kb.op(PE, lambda e: e.matmul(out, lhsT, rhs, start=start, stop=stop), reads, writes)


def tpose(kb, out, in_, ident, reads, writes):
    return kb.op(PE, lambda e: e.transpose(out, in_, ident), reads, writes)


def act(kb, out, in_, func, reads, writes, scale=None, bias=None, accum=None):
    kw = {}
    if scale is not None:
        kw["scale"] = scale
    if bias is not None:
        kw["bias"] = bias
    if accum is not None:
        kw["accum_out"] = accum
    return kb.op(ACT, lambda e: e.activation(out=out, in_=in_, func=func, **kw), reads, writes)


def tt(kb, eng, out, in0, in1, op, reads, writes):
    return kb.op(eng, lambda e: e.tensor_tensor(out, in0, in1, op), reads, writes)


def ts(kb, eng, out, in0, s1, op0, reads, writes, s2=None, op1=None):
    if op1 is None:
        return kb.op(eng, lambda e: e.tensor_scalar(out=out, in0=in0, scalar1=s1, scalar2=None, op0=op0), reads, writes)
    return kb.op(eng, lambda e: e.tensor_scalar(out=out, in0=in0, scalar1=s1, scalar2=s2, op0=op0, op1=op1), reads, writes)


def stt(kb, out, in0, scalar, in1, op0, op1, reads, writes):
    return kb.op(DVE, lambda e: e.scalar_tensor_tensor(out=out, in0=in0, scalar=scalar, in1=in1, op0=op0, op1=op1), reads, writes)


def cp(kb, eng, out, in_, reads, writes):
    return kb.op(eng, lambda e: e.tensor_copy(out, in_), reads, writes)


def mset(kb, eng, out, val, writes):
    return kb.op(eng, lambda e: e.memset(out, val), (), writes)


def recip(kb, out, in_, reads, writes):
    return kb.op(DVE, lambda e: e.reciprocal(out, in_), reads, writes)


class Ctx:
    def __init__(self, kb, ident, masks=None):
        self.kb = kb
        self.A = Arena(kb, 204 * 1024)
        A = self.A
        self.tp = [kb.psum("tp%d" % i, [128, 1024], BF16) for i in range(2)]
        self.pp = [kb.psum("pp%d" % i, [128, 1024], F32) for i in range(3)]
        self.pb = [self.pp[i // 2][:, (i % 2) * 512:(i % 2) * 512 + 512] for i in range(6)]
        self.identb = A.alloc([128, 128], BF16)
        self.identf = A.alloc([128, 128], F32)
        self.eps = A.alloc([128, 4], F32)
        self.Bconst = Buf("const")
        kb.dma(SP, self.identf, ident, writes=[self.Bconst])
        cp(kb, DVE, self.identb, self.identf, [self.Bconst], [self.Bconst])
        mset(kb, DVE, self.eps[:, 0:1], EPS, [self.Bconst])
        mset(kb, DVE, self.eps[:, 1:2], 4.0 * EPS, [self.Bconst])
        mset(kb, DVE, self.eps[:, 2:3], -PI, [self.Bconst])
        if masks is not None:
            self.mk = A.alloc([128, 2, 1024], F32)
            kb.dma(SP, self.mk, masks.rearrange("v p n -> p v n"), writes=[self.Bconst])
        self.base = A.off
        self.new_phase()

    def new_phase(self):
        self.A.off = self.base
        self.Btp = bufs("tp", 2)
        self.Bpb = bufs("pb", 6)


def rms_stats(kb, st, Bst, ss_col, out_col, tmp_col, scale, bias_ap, Bconst):
    act(kb, st[:, tmp_col:tmp_col + 1], st[:, ss_col:ss_col + 1], AF.Sqrt, [Bst, Bconst], [Bst], scale=scale, bias=bias_ap)
    recip(kb, st[:, out_col:out_col + 1], st[:, tmp_col:tmp_col + 1], [Bst], [Bst])


def emit_norm_transpose(kb, C, x_in, Bxin, tok0, nsub, gpre, Bgpre, hT, BhT, xs, Bxs, hb, Bhb, st, Bst, cnt):
    for sub in range(nsub):
        s = cnt["x"] % 2; cnt["x"] += 1
        r0 = tok0 + sub * 128
        kb.dma(SP, xs[s], x_in[r0:r0 + 128, :], reads=[Bxin], writes=[Bxs[s]])
        act(kb, hb[s], xs[s], AF.Square, [Bxs[s]], [Bhb[s], Bst[s]], accum=st[s][:, 0:1])
        rms_stats(kb, st[s], Bst[s], 0, 2, 1, 1.0 / D, C.eps[:, 0:1], C.Bconst)
        stt(kb, hb[s], xs[s], st[s][:, 2:3], gpre, ALU.mult, ALU.mult, [Bxs[s], Bst[s], Bgpre], [Bhb[s]])
        for j in range(2):
            p = cnt["tp"] % 2; cnt["tp"] += 1
            for q in range(4):
                dc = j * 4 + q
                tpose(kb, C.tp[p][:, q * 128:(q + 1) * 128], hb[s][:, dc * 128:(dc + 1) * 128], C.identb,
                      [Bhb[s], C.Bconst], [C.Btp[p]])
            src = C.tp[p][:, 0:512].rearrange("p (a b) -> p a b", a=4)
            dst = hT[:, j * 4:(j + 1) * 4, sub * 128:(sub + 1) * 128]
            if j == 0:
                act(kb, dst, src, AF.Copy, [C.Btp[p]], [BhT[sub]])
            else:
                cp(kb, DVE, dst, src, [C.Btp[p]], [BhT[sub]])


def emit_ffn(kb, C, x_in, x_out, gpre_d, gpost_d, wg, wu, wd, Bxin, Bxout, ntok=NTOK):
    C.new_phase()
    A = C.A
    TT = 1024
    ntile = ntok // TT
    gpre = A.alloc([128, D], F32); Bgpre = Buf("gpre")
    gpost = A.alloc([128, D], F32); Bgpost = Buf("gpost")
    xs = [A.alloc([128, D], F32) for _ in range(2)]; Bxs = bufs("xs", 2)
    hb = [A.alloc([128, D], BF16) for _ in range(2)]; Bhb = bufs("hb", 2)
    hT = A.alloc([128, 8, TT], BF16); BhT = bufs("hT", 8)
    actt = A.alloc([128, NFC, TT], BF16); Bact = bufs("act", NFC)
    stg = [A.alloc([128, 8, 128], F32) for _ in range(2)]; Bstg = bufs("stg", 2)
    stu = [A.alloc([128, 8, 128], F32) for _ in range(2)]; Bstu = bufs("stu", 2)
    wgb = [A.alloc([128, 8, 128], BF16) for _ in range(2)]; Bwg = bufs("wg", 2)
    wub = [A.alloc([128, 8, 128], BF16) for _ in range(2)]; Bwu = bufs("wu", 2)
    std = [A.alloc([128, D], F32) for _ in range(2)]; Bstd = bufs("std", 2)
    wdb = A.alloc([128, NFC, D], BF16); Bwd = bufs("wd", NFC)
    sg = [A.alloc([128, 512], BF16) for _ in range(2)]; Bsg = bufs("sg", 2)
    ys = [A.alloc([128, D], F32) for _ in range(2)]; Bys = bufs("ys", 2)
    xr = [A.alloc([128, D], F32) for _ in range(2)]; Bxr = bufs("xr", 2)
    st = [A.alloc([128, 8], F32) for _ in range(2)]; Bst = bufs("st", 2)

    kb.dma(SP, gpre, gpre_d.partition_broadcast(128), writes=[Bgpre])
    kb.dma(SP, gpost, gpost_d.partition_broadcast(128), writes=[Bgpost])
    for fc in range(NFC):
        s = fc % 2
        kb.dma(SP, std[s], wd[fc * 128:(fc + 1) * 128, :], writes=[Bstd[s]])
        cp(kb, POOL, wdb[:, fc, :], std[s], [Bstd[s]], [Bwd[fc]])
    wg_v = wg.rearrange("(dc dp) f -> dp dc f", dp=128)
    wu_v = wu.rearrange("(dc dp) f -> dp dc f", dp=128)
    cnt = {"x": 0, "w": 0, "gu": 0, "tp": 0, "sg": 0, "y": 0, "py": 0}
    for tt_ in range(ntile):
        t0 = tt_ * TT
        emit_norm_transpose(kb, C, x_in, Bxin, t0, 8, gpre, Bgpre, hT, BhT, xs, Bxs, hb, Bhb, st, Bst, cnt)
        for fc in range(NFC):
            s = cnt["w"] % 2; cnt["w"] += 1
            kb.dma(SP, stg[s], wg_v[:, :, fc * 128:(fc + 1) * 128], writes=[Bstg[s]])
            kb.dma(SP, stu[s], wu_v[:, :, fc * 128:(fc + 1) * 128], writes=[Bstu[s]])
            cp(kb, POOL, wgb[s], stg[s], [Bstg[s]], [Bwg[s]])
            cp(kb, POOL, wub[s], stu[s], [Bstu[s]], [Bwu[s]])
            for half in range(2):
                g = cnt["gu"] % 2; cnt["gu"] += 1
                pg, pu = C.pb[2 * g], C.pb[2 * g + 1]
                Bpg, Bpu = C.Bpb[2 * g], C.Bpb[2 * g + 1]
                c0 = half * 512
                hsubs = BhT[half * 4:(half + 1) * 4]
                for dc in range(8):
                    mm(kb, pg, wgb[s][:, dc, :], hT[:, dc, c0:c0 + 512], dc == 0, dc == 7, [Bwg[s]] + hsubs, [Bpg])
                for dc in range(8):
                    mm(kb, pu, wub[s][:, dc, :], hT[:, dc, c0:c0 + 512], dc == 0, dc == 7, [Bwu[s]] + hsubs, [Bpu])
                q = cnt["sg"] % 2; cnt["sg"] += 1
                act(kb, sg[q], pg, AF.Silu, [Bpg], [Bsg[q]])
                tt(kb, DVE, actt[:, fc, c0:c0 + 512], sg[q], pu, ALU.mult, [Bsg[q], Bpu], [Bact[fc]])
        for sub in range(8):
            r0 = t0 + sub * 128
            s = cnt["y"] % 2; cnt["y"] += 1
            kb.dma(SP, xr[s], x_in[r0:r0 + 128, :], reads=[Bxin], writes=[Bxr[s]])
            g = cnt["py"] % 3; cnt["py"] += 1
            pys = [C.pb[2 * g], C.pb[2 * g + 1]]
            Bpys = [C.Bpb[2 * g], C.Bpb[2 * g + 1]]
            for half in range(2):
                for fc in range(NFC):
                    mm(kb, pys[half], actt[:, fc, sub * 128:(sub + 1) * 128], wdb[:, fc, half * 512:(half + 1) * 512],
                       fc == 0, fc == NFC - 1, [Bact[fc], Bwd[fc]], [Bpys[half]])
            emit_postnorm_residual(kb, C, pys, Bpys, ys[s], Bys[s], xr[s], Bxr[s], st[s], Bst[s], gpost, Bgpost, 4.0 / D, C.eps[:, 1:2])
            kb.dma(SP, x_out[r0:r0 + 128, :], xr[s], reads=[Bxr[s]], writes=[Bxout])
    kb.barrier()


def emit_postnorm_residual(kb, C, pys, Bpys, ys, Bys, xr, Bxr, st, Bst, gpost, Bgpost, scale, bias_ap):
    for half in range(2):
        act(kb, ys[:, half * 512:(half + 1) * 512], pys[half], AF.Square, [Bpys[half]], [Bys, Bst],
            accum=st[:, 5 + half:6 + half])
    tt(kb, DVE, st[:, 7:8], st[:, 5:6], st[:, 6:7], ALU.add, [Bst], [Bst])
    rms_stats(kb, st, Bst, 7, 2, 1, scale, bias_ap, C.Bconst)
    for half in range(2):
        stt(kb, ys[:, half * 512:(half + 1) * 512], pys[half], st[:, 2:3], gpost[:, half * 512:(half + 1) * 512],
            ALU.mult, ALU.mult, [Bpys[half], Bst, Bgpost], [Bys])
    tt(kb, POOL, xr, xr, ys, ALU.add, [Bys, Bxr], [Bxr])


INCOLS = 5632
NPAR = 1080
PC_LR, PC_LI, PC_LDT, PC_D, PC_CW, PC_BR, PC_BI, PC_CR, PC_CI = 0, 16, 32, 48, 50, 56, 312, 568, 824
I32 = mybir.dt.int32
SEQ = 2048


class WChunk:
    def __init__(self, kb, A, w_in, nslot=2):
        self.kb = kb
        self.wv = w_in.rearrange("(dc dp) f -> dp dc f", dp=128)
        self.st = [A.alloc([128, 8, 128], F32) for _ in range(2)]; self.Bst = bufs("wst", 2)
        self.wb = [A.alloc([128, 8, 128], BF16) for _ in range(nslot)]; self.Bwb = bufs("wbf", nslot)
        self.n = 0
        self.nslot = nslot

    def load(self, col0):
        s = self.n % 2
        b = self.n % self.nslot
        self.n += 1
        self.kb.dma(SP, self.st[s], self.wv[:, :, col0:col0 + 128], writes=[self.Bst[s]])
        cp(self.kb, POOL, self.wb[b], self.st[s], [self.Bst[s]], [self.Bwb[b]])
        return self.wb[b], self.Bwb[b]


def emit_mixer_params(kb, C, par, Bpar, prm, Bprm, pw, dl, BT, CT, Bbc, tmpA, tmpB, Btmp, Bx, BBx):
    def P(i):
        return prm[:, i, :]
    R, Wr = [Bprm, Bpar], [Bprm]
    act(kb, P(0), par[:, PC_LDT:PC_LDT + 16], AF.Exp, R, Wr)
    ts(kb, DVE, P(1), par[:, PC_LR:PC_LR + 16], -1e-4, ALU.min, R, Wr)
    tt(kb, DVE, P(2), P(1), P(0), ALU.mult, R, Wr)
    act(kb, P(3), P(2), AF.Exp, R, Wr)
    tt(kb, DVE, P(4), par[:, PC_LI:PC_LI + 16], P(0), ALU.mult, R, Wr)
    ts(kb, DVE, P(5), P(4), 1.0 / (2 * PI), ALU.mult, R, Wr)
    cp(kb, DVE, P(6).bitcast(I32), P(5), R, Wr)
    cp(kb, DVE, P(7), P(6).bitcast(I32), R, Wr)
    stt(kb, P(8), P(7), -2 * PI, P(4), ALU.mult, ALU.add, R, Wr)
    ts(kb, DVE, P(9), P(8), PI, ALU.is_gt, R, Wr, s2=2 * PI, op1=ALU.mult)
    tt(kb, DVE, P(8), P(8), P(9), ALU.subtract, R, Wr)
    ts(kb, DVE, P(9), P(8), -PI, ALU.is_lt, R, Wr, s2=2 * PI, op1=ALU.mult)
    tt(kb, DVE, P(8), P(8), P(9), ALU.add, R, Wr)
    act(kb, pw[:, 0, 1, :], P(8), AF.Sin, R, Wr)
    ts(kb, DVE, P(11), P(8), PI / 2, ALU.add, R, Wr)
    ts(kb, DVE, P(9), P(11), PI, ALU.is_gt, R, Wr, s2=2 * PI, op1=ALU.mult)
    tt(kb, DVE, P(11), P(11), P(9), ALU.subtract, R, Wr)
    act(kb, pw[:, 0, 0, :], P(11), AF.Sin, R, Wr)
    tt(kb, DVE, P(13), P(3), pw[:, 0, 0, :], ALU.mult, R, Wr)
    tt(kb, DVE, P(14), P(3), pw[:, 0, 1, :], ALU.mult, R, Wr)
    tt(kb, DVE, P(15), P(1), P(1), ALU.mult, R, Wr)
    tt(kb, DVE, P(16), par[:, PC_LI:PC_LI + 16], par[:, PC_LI:PC_LI + 16], ALU.mult, R, Wr)
    tt(kb, DVE, P(15), P(15), P(16), ALU.add, R, Wr)
    recip(kb, P(15), P(15), R, Wr)
    ts(kb, DVE, P(16), P(13), -1.0, ALU.add, R, Wr)
    tt(kb, DVE, P(17), P(16), P(1), ALU.mult, R, Wr)
    tt(kb, DVE, P(18), P(14), par[:, PC_LI:PC_LI + 16], ALU.mult, R, Wr)
    tt(kb, DVE, P(17), P(17), P(18), ALU.add, R, Wr)
    tt(kb, DVE, P(17), P(17), P(15), ALU.mult, R, Wr)
    tt(kb, DVE, P(18), P(14), P(1), ALU.mult, R, Wr)
    tt(kb, DVE, P(19), P(16), par[:, PC_LI:PC_LI + 16], ALU.mult, R, Wr)
    tt(kb, DVE, P(18), P(18), P(19), ALU.subtract, R, Wr)
    tt(kb, DVE, P(18), P(18), P(15), ALU.mult, R, Wr)
    ts(kb, DVE, P(9), P(8), 0.0, ALU.is_lt, R, Wr, s2=2 * PI, op1=ALU.mult)
    tt(kb, DVE, dl[:, 0, :], P(8), P(9), ALU.add, R, Wr)
    for k in range(10):
        ts(kb, DVE, P(20), dl[:, k, :], 2.0, ALU.mult, R, Wr)
        ts(kb, DVE, P(21), P(20), 2 * PI, ALU.is_ge, R, Wr)
        stt(kb, dl[:, k + 1, :], P(21), -2 * PI, P(20), ALU.mult, ALU.add, R, Wr)
    fre = P(17).unsqueeze(2).broadcast_to([128, 16, 16])
    fim = P(18).unsqueeze(2).broadcast_to([128, 16, 16])
    br = par[:, PC_BR:PC_BR + 256].rearrange("p (c h) -> p c h", c=16)
    bi = par[:, PC_BI:PC_BI + 256].rearrange("p (c h) -> p c h", c=16)
    tA = tmpA[:, 0:256].rearrange("p (c h) -> p c h", c=16)
    tB = tmpB[:, 0:256].rearrange("p (c h) -> p c h", c=16)
    tC = tmpA[:, 256:512].rearrange("p (c h) -> p c h", c=16)
    tD = tmpB[:, 256:512].rearrange("p (c h) -> p c h", c=16)
    RT = [Bprm, Bpar, Btmp]
    tt(kb, DVE, tA, br, fre, ALU.mult, RT, [Btmp])
    tt(kb, DVE, tB, bi, fim, ALU.mult, RT, [Btmp])
    tt(kb, DVE, tA, tA, tB, ALU.subtract, RT, [Btmp])
    tt(kb, DVE, tC, bi, fre, ALU.mult, RT, [Btmp])
    tt(kb, DVE, tD, br, fim, ALU.mult, RT, [Btmp])
    tt(kb, DVE, tC, tC, tD, ALU.add, RT, [Btmp])
    n = 0
    for ri, src in enumerate([tA, tC]):
        for col in range(16):
            po = 32 * (col % 4)
            s = n % 2; n += 1
            mset(kb, DVE, Bx[s], 0.0, [BBx[s]])
            cp(kb, DVE, Bx[s][0:64, po:po + 16], src[0:64, col, :], [Btmp], [BBx[s]])
            cp(kb, DVE, Bx[s][64:128, po + 16:po + 32], src[64:128, col, :], [Btmp], [BBx[s]])
            pbk = 4 + s
            mm(kb, C.pb[pbk][:, 0:128], Bx[s], C.identb, True, True, [BBx[s], C.Bconst], [C.Bpb[pbk]])
            act(kb, BT[:, ri * 16 + col, :], C.pb[pbk][:, 0:128], AF.Copy, [C.Bpb[pbk]], [Bbc])
    mset(kb, DVE, CT, 0.0, [Bbc])
    cr = par[:, PC_CR:PC_CR + 256].rearrange("p (c h) -> p c h", c=16)
    ci = par[:, PC_CI:PC_CI + 256].rearrange("p (c h) -> p c h", c=16)
    for kind, (src, sgn) in enumerate([(cr, 1.0), (cr, -1.0), (ci, -1.0)]):
        for col in range(16):
            po = 32 * (col % 4)
            ts(kb, DVE, CT[0:64, kind * 16 + col, po:po + 16], src[0:64, col, :], sgn, ALU.mult, [Bpar], [Bbc])
            ts(kb, DVE, CT[64:128, kind * 16 + col, po + 16:po + 32], src[64:128, col, :], sgn, ALU.mult, [Bpar], [Bbc])


def emit_mixer(kb, C, x_in, x_out, W, Bxin, Bxout, nseq=2, dbg=None):
    C.new_phase()
    A = C.A
    par = A.alloc([128, NPAR], F32); Bpar = Buf("par")
    prm = A.alloc([128, 24, 16], F32); Bprm = Buf("prm")
    pw = A.alloc([128, 1, 2, 16], F32)
    dl = A.alloc([128, 11, 16], F32)
    Btsc = bufs('tabscr', 16)
    BT = A.alloc([128, 32, 128], BF16)
    CT = A.alloc([128, 48, 128], BF16); Bbc = Buf("bc")
    hT = A.alloc([128, 8, SEQ], BF16); BhT = bufs("hT", 16)
    ca = A.alloc([128, 2, SEQ], BF16); Bca = bufs("ca", 2)
    z = A.alloc([128, 2, SEQ], BF16); Bz = bufs("z", 2)
    st = [A.alloc([128, 8], F32) for _ in range(2)]; Bst = bufs("st", 2)
    uo_off = A.off
    u32 = A.alloc([128, 2, SEQ], F32)
    ubf = A.alloc([128, 2, SEQ], BF16)
    A.off = uo_off
    oT = A.alloc([128, 4, SEQ], BF16)
    A.off = max(A.off, uo_off + (2 * SEQ * 4 + 2 * SEQ * 2) // 4)
    tmpA = A.alloc([128, 512], F32); tmpB = A.alloc([128, 512], F32); Btmp = Buf("ptmp")
    Bx = [A.alloc([128, 128], BF16) for _ in range(2)]; BBx = bufs("Bx", 2)
    local0 = A.off

    kb.dma(SP, par, W["par"], writes=[Bpar])

    cnt = {"x": 0, "tp": 0, "pb": 0, "tab": 0, "m": 0, "S": 0, "o": 0, "g": 0, "y": 0}

    def nextpb(n=6, base=0):
        i = base + cnt["pb"] % n
        cnt["pb"] += 1
        return C.pb[i], C.Bpb[i]

    for sq in range(nseq):
        tok0 = sq * SEQ
        Bu = bufs("u", 2)
        BoT = Buf("oT")
        if sq > 0:
            C.new_phase()
        A.off = local0
        gpre = A.alloc([128, D], F32); Bgpre = Buf("gpre")
        kb.dma(SP, gpre, W["gpre"].partition_broadcast(128), writes=[Bgpre])
        xs = [A.alloc([128, D], F32) for _ in range(2)]; Bxs = bufs("xs", 2)
        hb = [A.alloc([128, D], BF16) for _ in range(2)]; Bhb = bufs("hb", 2)
        wc = WChunk(kb, A, W["w_in"])
        ab = A.alloc([128, SEQ], F32); Bab = Buf("ab")
        ac = A.alloc([128, SEQ], F32); Bac = Buf("ac")
        cc = A.alloc([128, SEQ], F32); Bcc = Buf("cc")
        emit_norm_transpose(kb, C, x_in, Bxin, tok0, 16, gpre, Bgpre, hT, BhT, xs, Bxs, hb, Bhb, st, Bst, cnt)
        if sq == 0:
            emit_mixer_params(kb, C, par, Bpar, prm, Bprm, pw, dl, BT, CT, Bbc, tmpA, tmpB, Btmp, Bx, BBx)

        def proj(wt, Bwt, tile):
            ps, Bps = nextpb()
            hs = BhT[tile * 4:(tile + 1) * 4]
            for dc in range(8):
                mm(kb, ps, wt[:, dc, :], hT[:, dc, tile * 512:(tile + 1) * 512], dc == 0, dc == 7, [Bwt] + hs, [Bps])
            return ps, Bps

        for cch in range(2):
            wt, Bwt = wc.load(0 + 128 * cch)
            for tile in range(4):
                ps, Bps = proj(wt, Bwt, tile)
                act(kb, ab[:, tile * 512:(tile + 1) * 512], ps, AF.Copy, [Bps], [Bab])
            wt, Bwt = wc.load(256 + 128 * cch)
            for tile in range(4):
                ps, Bps = proj(wt, Bwt, tile)
                act(kb, ac[:, tile * 512:(tile + 1) * 512], ps, AF.Copy, [Bps], [Bac])
            wt, Bwt = wc.load(512 + 128 * cch)
            for tile in range(4):
                ps, Bps = proj(wt, Bwt, tile)
                sl = slice(tile * 512, (tile + 1) * 512)
                tt(kb, DVE, ac[:, sl], ac[:, sl], ps, ALU.mult, [Bac, Bps], [Bac])
            cw = PC_CW + 3 * cch
            ts(kb, DVE, cc, ac, par[:, cw + 1:cw + 2], ALU.mult, [Bac, Bpar], [Bcc])
            stt(kb, cc[:, 1:SEQ], ac[:, 0:SEQ - 1], par[:, cw:cw + 1], cc[:, 1:SEQ], ALU.mult, ALU.add, [Bac, Bpar, Bcc], [Bcc])
            stt(kb, cc[:, 0:SEQ - 1], ac[:, 1:SEQ], par[:, cw + 2:cw + 3], cc[:, 0:SEQ - 1], ALU.mult, ALU.add, [Bac, Bpar, Bcc], [Bcc])
            tt(kb, DVE, ca[:, cch, :], ab, cc, ALU.mult, [Bab, Bcc], [Bca[cch]])
        for uc in range(2):
            wt, Bwt = wc.load(768 + 128 * uc)
            for tile in range(4):
                ps, Bps = proj(wt, Bwt, tile)
                sl = slice(tile * 512, (tile + 1) * 512)
                act(kb, u32[:, uc, sl], ps, AF.Copy, [Bps], [Bu[uc]])
                cp(kb, DVE, ubf[:, uc, sl], ps, [Bps], [Bu[uc]])
        if dbg is not None and sq == 0:
            kb.dma(SP, dbg["ca"].rearrange("c p t -> p c t"), ca, reads=Bca, writes=[Buf("dbgo")])
            kb.dma(SP, dbg["u"].rearrange("c p t -> p c t"), u32, reads=Bu, writes=[Buf("dbgo2")])
        kb.barrier()

        C.new_phase(); A.off = local0
        cosT = [A.alloc([128, SEQ], F32) for _ in range(2)]
        sinT = [A.alloc([128, SEQ], F32) for _ in range(2)]; Btab = bufs("tab", 2)
        tab_tmp = A.alloc([128, SEQ], F32); ta = tab_tmp[:, 0:1024]; tb = tab_tmp[:, 1024:2048]; Btt = Buf("ttmp")
        mtmp = [A.alloc([128, 4, 512], F32)] * 2; Bm = [Buf("mtmp")] * 2
        btr = A.alloc([128, SEQ], F32); bti = A.alloc([128, SEQ], F32); Bbt = bufs("bt", 2)
        str_ = A.alloc([128, SEQ], F32); sti = A.alloc([128, SEQ], F32); Bsst = bufs("sst", 2)
        prd = [A.alloc([128, 4, 512], BF16) for _ in range(2)]; Bprd = bufs("prd", 2)
        yt = [A.alloc([128, 512], F32) for _ in range(2)]; Byt = bufs("yt", 2)
        for uc in range(2):
            first = True
            for ptl in range(4):
                pt = 4 * uc + ptl
                for dr in range(2):
                    col = dr * 8 + pt
                    tb_i = cnt["tab"] % 2; cnt["tab"] += 1
                    cT, sT, Bt = cosT[tb_i], sinT[tb_i], Btab[tb_i]
                    if sq == 0:
                        RT = [Bt, Btt, Bprm]
                        mset(kb, DVE, sT[:, 0:1], 0.0, [Bt])
                        for k in range(11):
                            n = 1 << k
                            dk = dl[:, k, col:col + 1]
                            ts(kb, DVE, ta[:, 0:n], sT[:, 0:n], dk, ALU.add, RT, [Btt])
                            ts(kb, DVE, tb[:, 0:n], ta[:, 0:n], 2 * PI, ALU.is_ge, RT, [Btt])
                            stt(kb, sT[:, n:2 * n], tb[:, 0:n], -2 * PI, ta[:, 0:n], ALU.mult, ALU.add, RT, [Bt])
                        act(kb, cT, sT, AF.Sin, [Bt], [Bt], scale=0.5)
                        act(kb, cT, cT, AF.Square, [Bt], [Bt], scale=1.4142135623730951)
                        kb.op(ACT, (lambda o=cT: lambda e: e.add(o, o, -1.0))(), [Bt], [Bt])
                        act(kb, sT, sT, AF.Sin, [Bt, C.Bconst], [Bt], bias=C.eps[:, 2:3])
                        kb.dma(SP, W["tabscr"][col, 0], cT, reads=[Bt], writes=[Btsc[col]])
                        kb.dma(SP, W["tabscr"][col, 1], sT, reads=[Bt], writes=[Btsc[col]])
                    else:
                        kb.dma(SP, cT, W["tabscr"][col, 0], reads=[Btsc[col]], writes=[Bt])
                        kb.dma(SP, sT, W["tabscr"][col, 1], reads=[Btsc[col]], writes=[Bt])

                    def tview(tab, tile):
                        if dr == 0:
                            return tab[:, tile * 512:(tile + 1) * 512]
                        lo = SEQ - (tile + 1) * 512
                        return tab[:, lo:lo + 512][:, ::-1]
                    for tile in range(4):
                        sl = slice(tile * 512, (tile + 1) * 512)
                        pr, Bpr = C.pb[4], C.Bpb[4]
                        pi_, Bpi = C.pb[5], C.Bpb[5]
                        mm(kb, pr, BT[:, col, :], ubf[:, uc, sl], True, True, [Bbc, Bu[uc]], [Bpr])
                        mm(kb, pi_, BT[:, 16 + col, :], ubf[:, uc, sl], True, True, [Bbc, Bu[uc]], [Bpi])
                        m = cnt["m"] % 2; cnt["m"] += 1
                        mt = mtmp[m]
                        tt(kb, DVE, mt[:, 0, :], pr, tview(cT, tile), ALU.mult, [Bpr, Bt], [Bm[m]])
                        tt(kb, DVE, mt[:, 1, :], pi_, tview(sT, tile), ALU.mult, [Bpi, Bt], [Bm[m]])
                        tt(kb, DVE, mt[:, 2, :], pi_, tview(cT, tile), ALU.mult, [Bpi, Bt], [Bm[m]])
                        tt(kb, DVE, mt[:, 3, :], pr, tview(sT, tile), ALU.mult, [Bpr, Bt], [Bm[m]])
                        tt(kb, DVE, btr[:, sl], mt[:, 0, :], mt[:, 1, :], ALU.add, [Bm[m]], [Bbt[0]])
                        tt(kb, DVE, bti[:, sl], mt[:, 2, :], mt[:, 3, :], ALU.subtract, [Bm[m]], [Bbt[1]])
                    dec = prm[:, 3, col:col + 1].to_broadcast([128, SEQ])
                    if dr == 0:
                        kb.op(DVE, (lambda o=str_, d=dec, i=btr: lambda e: e.tensor_tensor_scan(o, d, i, 0.0, ALU.mult, ALU.add))(),
                              [Bbt[0], Bprm], [Bsst[0]])
                        kb.op(DVE, (lambda o=sti, d=dec, i=bti: lambda e: e.tensor_tensor_scan(o, d, i, 0.0, ALU.mult, ALU.add))(),
                              [Bbt[1], Bprm], [Bsst[1]])
                    else:
                        kb.op(DVE, (lambda o=str_[:, ::-1], d=dec, i=btr[:, ::-1]: lambda e: e.tensor_tensor_scan(o, d, i, 0.0, ALU.mult, ALU.add))(),
                              [Bbt[0], Bprm], [Bsst[0]])
                        kb.op(DVE, (lambda o=sti[:, ::-1], d=dec, i=bti[:, ::-1]: lambda e: e.tensor_tensor_scan(o, d, i, 0.0, ALU.mult, ALU.add))(),
                              [Bbt[1], Bprm], [Bsst[1]])
                    last = (ptl == 3 and dr == 1)
                    for tile in range(4):
                        sl = slice(tile * 512, (tile + 1) * 512)
                        m = cnt["m"] % 2; cnt["m"] += 1
                        pd = prd[m]
                        pe_ = DVE if (sq == 1 and tile == 3) else POOL
                        tt(kb, pe_, pd[:, 0, :], tview(cT, tile), str_[:, sl], ALU.mult, [Bt, Bsst[0]], [Bprd[m]])
                        tt(kb, pe_, pd[:, 1, :], tview(sT, tile), sti[:, sl], ALU.mult, [Bt, Bsst[1]], [Bprd[m]])
                        tt(kb, pe_, pd[:, 2, :], tview(sT, tile), str_[:, sl], ALU.mult, [Bt, Bsst[0]], [Bprd[m]])
                        tt(kb, pe_, pd[:, 3, :], tview(cT, tile), sti[:, sl], ALU.mult, [Bt, Bsst[1]], [Bprd[m]])
                        py, Bpy = C.pb[tile], C.Bpb[tile]
                        kinds = [0, 1, 2, 2]
                        for j in range(4):
                            mm(kb, py, CT[:, kinds[j] * 16 + col, :], pd[:, j, :], first and j == 0, last and j == 3,
                               [Bbc, Bprd[m]], [Bpy])
                    first = False
            for tile in range(4):
                sl = slice(tile * 512, (tile + 1) * 512)
                q = cnt["y"] % 2; cnt["y"] += 1
                stt(kb, yt[q], u32[:, uc, sl], par[:, PC_D + uc:PC_D + uc + 1], C.pb[tile], ALU.mult, ALU.add,
                    [Bu[uc], Bpar, C.Bpb[tile]], [Byt[q]])
                act(kb, z[:, uc, sl], yt[q], AF.Gelu, [Byt[q]], [Bz[uc]])
                if dbg is not None and sq == 0:
                    kb.dma(SP, dbg["y"][uc, :, sl], yt[q], reads=[Byt[q]], writes=[Buf("dbgy")])
        kb.barrier()

        C.new_phase(); A.off = local0
        wc = WChunk(kb, A, W["w_in"])
        Eb = [A.alloc([128, 640], BF16) for _ in range(2)]; BEb = bufs("Eb", 2)
        qz = [[A.alloc([128, SEQ], BF16) for _ in range(2)] for _ in range(2)]; Bqb = bufs("qb", 2)
        for i in range(2):
            for j in range(2):
                mset(kb, POOL, qz[i][j], 0.0, [Bqb[i]])
            mset(kb, POOL, Eb[i][64:128, 512:640], 0.0, [BEb[i]])
        kbf = [A.alloc([128, SEQ], BF16) for _ in range(2)]; Bkb = bufs("kb", 2)
        Vt = [A.alloc([128, 16, 2, 65], BF16) for _ in range(2)]; BVt = bufs("Vt", 2)
        gst = [A.alloc([128, 1024], F32) for _ in range(2)]; Bgst = bufs("gst", 2)
        Gt = [A.alloc([128, 2, 1024], BF16) for _ in range(2)]; BGt = bufs("Gt", 2)
        otok = A.alloc([128, 16, 512], BF16); Bot = bufs("otok", 16)
        rc = [A.alloc([128, 2], F32) for _ in range(2)]; Brc = bufs("rc", 2)
        for i in range(2):
            mset(kb, DVE, Vt[i][:, :, :, 64:65], 1.0, [BVt[i]])
        for hp in range(4):
            s = hp % 2
            wt, Bwt = wc.load(1024 + 128 * hp)
            for tile in range(4):
                ps, Bps = proj(wt, Bwt, tile)
                act(kb, qz[s][0][0:64, tile * 512:(tile + 1) * 512], ps[0:64, :], AF.Copy, [Bps], [Bqb[s]], scale=0.125)
                act(kb, qz[s][1][64:128, tile * 512:(tile + 1) * 512], ps[64:128, :], AF.Copy, [Bps], [Bqb[s]], scale=0.125)
            wt, Bwt = wc.load(1536 + 128 * hp)
            for tile in range(4):
                ps, Bps = proj(wt, Bwt, tile)
                cp(kb, DVE, kbf[s][:, tile * 512:(tile + 1) * 512], ps, [Bps], [Bkb[s]])
            wt, Bwt = wc.load(2048 + 128 * hp)
            for j in range(4):
                ps, Bps = nextpb()
                for q in range(4):
                    sub = 4 * j + q
                    for dc in range(8):
                        mm(kb, ps[:, q * 128:(q + 1) * 128], hT[:, dc, sub * 128:(sub + 1) * 128], wt[:, dc, :], dc == 0, dc == 7,
                           [Bwt, BhT[sub]], [Bps])
                psv = ps.rearrange("p (a b) -> p a b", a=4)
                act(kb, Vt[s][:, 4 * j:4 * j + 4, 0, 0:64], psv[:, :, 0:64], AF.Copy, [Bps], [BVt[s]])
                cp(kb, DVE, Vt[s][:, 4 * j:4 * j + 4, 1, 0:64], psv[:, :, 64:128], [Bps], [BVt[s]])
            for e_ in range(2):
                h = 2 * hp + e_
                pbase = 64 * e_
                g = cnt["g"] % 2; cnt["g"] += 1
                kb.dma(SP, gst[g], W["gt"][h], writes=[Bgst[g]])
                tt(kb, POOL, Gt[g][:, 0, :], gst[g], C.mk[:, 0, :], ALU.add, [Bgst[g], C.Bconst], [BGt[g]])
                tt(kb, POOL, Gt[g][:, 1, :], gst[g], C.mk[:, 1, :], ALU.add, [Bgst[g], C.Bconst], [BGt[g]])
                def na_front(blk):
                    r0 = 2 * blk
                    if blk < 2:
                        rs0, nrow, gv = 0, 8, 0
                    elif blk >= 14:
                        rs0, nrow, gv = 24, 8, 0
                    else:
                        rs0, nrow, gv = r0 - 4, 9, 1
                    nch = (nrow + 1) // 2
                    si = cnt["S"] % 2; cnt["S"] += 1
                    S = C.pp[si]
                    BS = [C.Bpb[2 * si], C.Bpb[2 * si + 1]]
                    for c in range(nch):
                        kn = 128 if 2 * c + 1 < nrow else 64
                        ksub = rs0 // 2 + c
                        base = rs0 + 2 * c - r0 + 7
                        e0 = 15 - base
                        assert 0 <= e0 <= 14
                        outp = S[0:kn, c * 128:(c + 1) * 128]
                        Bo = [BS[c // 4]]
                        mm(kb, outp, kbf[s][:, ksub * 128:ksub * 128 + kn], qz[s][e_][:, blk * 128:(blk + 1) * 128],
                           True, False, [Bkb[s], Bqb[s]], Bo)
                        mm(kb, outp, C.identb[:, 0:kn], Gt[g][:, gv, e0 * 64:e0 * 64 + 128], False, True,
                           [C.Bconst, BGt[g]], Bo)
                    Ebv = Eb[si]
                    nfull = nch if nrow % 2 == 0 else nch - 1
                    act(kb, Ebv[:, 0:nfull * 128], S[:, 0:nfull * 128], AF.Exp, [BS[0]], [BEb[si]])
                    if nfull < nch:
                        act(kb, Ebv[0:64, 512:640], S[0:64, 512:640], AF.Exp, [BS[1]], [BEb[si]])
                    return (blk, rs0, nrow, nch, si)

                def na_back(job):
                    blk, rs0, nrow, nch, si = job
                    Ebv = Eb[si]
                    oi = 4 + cnt["o"] % 2; cnt["o"] += 1
                    po, Bpo = C.pb[oi], C.Bpb[oi]
                    for c in range(nch):
                        ksub = rs0 // 2 + c
                        mm(kb, po[:, 0:65], Ebv[:, c * 128:(c + 1) * 128], Vt[s][:, ksub, e_, :], c == 0, c == nch - 1,
                           [BEb[si], BVt[s]], [Bpo])
                    ri = oi - 4
                    recip(kb, rc[ri][:, 0:1], po[:, 64:65], [Bpo], [Brc[ri]])
                    ts(kb, DVE, otok[:, blk, h * 64:(h + 1) * 64], po[:, 0:64], rc[ri][:, 0:1], ALU.mult, [Bpo, Brc[ri]], [Bot[blk]])

                prev = None
                for blk in range(16):
                    job = na_front(blk)
                    if prev is not None:
                        na_back(prev)
                    prev = job
                na_back(prev)
        for blk in range(16):
            p = cnt["tp"] % 2; cnt["tp"] += 1
            for q in range(4):
                tpose(kb, C.tp[p][:, q * 128:(q + 1) * 128], otok[:, blk, q * 128:(q + 1) * 128], C.identb,
                      [Bot[blk], C.Bconst], [C.Btp[p]])
            src = C.tp[p][:, 0:512].rearrange("p (a b) -> p a b", a=4)
            dst = oT[:, :, blk * 128:(blk + 1) * 128]
            if blk % 2 == 0:
                act(kb, dst, src, AF.Copy, [C.Btp[p]], [BoT])
            else:
                cp(kb, DVE, dst, src, [C.Btp[p]], [BoT])
        if dbg is not None and sq == 0:
            kb.dma(SP, dbg["o"].rearrange("(b p) c -> p b c", p=128), otok, reads=Bot, writes=[Buf("dbgo3")])
        kb.barrier()

        C.new_phase(); A.off = local0
        mixT = A.alloc([128, 8, SEQ], BF16); Bmix = bufs("mix", 8)
        merge_a0 = A.off
        wc = WChunk(kb, A, W["w_in"], nslot=6)
        wst = [A.alloc([128, D], F32) for _ in range(2)]; Bwst = bufs("wst2", 2)
        woa = A.alloc([128, 2, D], BF16); wga = A.alloc([128, 2, D], BF16); wgb = A.alloc([128, 2, D], BF16)
        woc = A.alloc([128, 4, D], BF16); Bsw = bufs("sw", 4)
        tm = [A.alloc([128, 512], F32) for _ in range(6)]; Btm = bufs("tm", 6)
        nst = 0
        for wi_, (wten, name, nchunk) in enumerate([(woa, "woa", 2), (wga, "wga", 2), (wgb, "wgb", 2), (woc, "woc", 4)]):
            for j in range(nchunk):
                s = nst % 2; nst += 1
                kb.dma(SP, wst[s], W[name][j * 128:(j + 1) * 128, :], writes=[Bwst[s]])
                cp(kb, POOL, wten[:, j, :], wst[s], [Bwst[s]], [Bsw[wi_]])
        for dcp in range(8):
            dsl = slice(dcp * 128, (dcp + 1) * 128)
            wga_, Bwga_ = wc.load(2560 + 128 * dcp)
            wgb_, Bwgb_ = wc.load(3584 + 128 * dcp)
            wgc_, Bwgc_ = wc.load(4608 + 128 * dcp)
            for tile in range(4):
                sl = slice(tile * 512, (tile + 1) * 512)
                k6 = cnt["m"] % 2; cnt["m"] += 1
                t0_, t1_, t2_ = tm[3 * k6], tm[3 * k6 + 1], tm[3 * k6 + 2]
                B0, B1, B2 = Btm[3 * k6], Btm[3 * k6 + 1], Btm[3 * k6 + 2]
                pga, Bpga = proj(wga_, Bwga_, tile)
                pya, Bpya = nextpb()
                for j in range(2):
                    mm(kb, pya, woa[:, j, dsl], ca[:, j, sl], j == 0, j == 1, [Bsw[0], Bca[j]], [Bpya])
                act(kb, t0_, pga, AF.Sigmoid, [Bpga], [B0])
                tt(kb, DVE, t1_, t0_, pya, ALU.mult, [B0, Bpya], [B1])
                pgb, Bpgb = proj(wgb_, Bwgb_, tile)
                py1, Bpy1 = nextpb()
                for j in range(2):
                    mm(kb, py1, wga[:, j, dsl], z[:, j, sl], j == 0, j == 1, [Bsw[1], Bz[j]], [Bpy1])
                py2, Bpy2 = nextpb()
                for j in range(2):
                    mm(kb, py2, wgb[:, j, dsl], z[:, j, sl], j == 0, j == 1, [Bsw[2], Bz[j]], [Bpy2])
                act(kb, t0_, py2, AF.Sigmoid, [Bpy2], [B0])
                tt(kb, DVE, t2_, t0_, py1, ALU.mult, [B0, Bpy1], [B2])
                act(kb, t0_, pgb, AF.Sigmoid, [Bpgb], [B0])
                tt(kb, DVE, t2_, t2_, t0_, ALU.mult, [B0, B2], [B2])
                tt(kb, DVE, t1_, t1_, t2_, ALU.add, [B1, B2], [B1])
                pgc, Bpgc = proj(wgc_, Bwgc_, tile)
                pyc, Bpyc = nextpb()
                for j in range(4):
                    mm(kb, pyc, woc[:, j, dsl], oT[:, j, sl], j == 0, j == 3, [Bsw[3], BoT], [Bpyc])
                act(kb, t0_, pgc, AF.Sigmoid, [Bpgc], [B0])
                tt(kb, DVE, t2_, t0_, pyc, ALU.mult, [B0, Bpyc], [B2])
                tt(kb, DVE, mixT[:, dcp, sl], t1_, t2_, ALU.add, [B1, B2], [Bmix[dcp]])
        kb.barrier()

        C.new_phase(); A.off = merge_a0
        gpost = A.alloc([128, D], F32); Bgpost = Buf("gpost")
        kb.dma(SP, gpost, W["gpost"].partition_broadcast(128), writes=[Bgpost])
        wst = [A.alloc([128, D], F32) for _ in range(2)]; Bwst = bufs("wst3", 2)
        wo = A.alloc([128, 8, D], BF16); Bwo = bufs("wo", 8)
        ys = [A.alloc([128, D], F32) for _ in range(2)]; Bys = bufs("ys", 2)
        xr = [A.alloc([128, D], F32) for _ in range(2)]; Bxr = bufs("xr", 2)
        for j in range(8):
            s = j % 2
            kb.dma(SP, wst[s], W["wo"][j * 128:(j + 1) * 128, :], writes=[Bwst[s]])
            cp(kb, POOL, wo[:, j, :], wst[s], [Bwst[s]], [Bwo[j]])
        for sub in range(16):
            r0 = tok0 + sub * 128
            s = sub % 2
            kb.dma(SP, xr[s], x_in[r0:r0 + 128, :], reads=[Bxin], writes=[Bxr[s]])
            g3 = cnt["y"] % 3; cnt["y"] += 1
            pys = [C.pb[2 * g3], C.pb[2 * g3 + 1]]
            Bpys = [C.Bpb[2 * g3], C.Bpb[2 * g3 + 1]]
            for half in range(2):
                for dc in range(8):
                    mm(kb, pys[half], mixT[:, dc, sub * 128:(sub + 1) * 128], wo[:, dc, half * 512:(half + 1) * 512],
                       dc == 0, dc == 7, [Bmix[dc], Bwo[dc]], [Bpys[half]])
            emit_postnorm_residual(kb, C, pys, Bpys, ys[s], Bys[s], xr[s], Bxr[s], st[s], Bst[s], gpost, Bgpost, 1.0 / D, C.eps[:, 0:1])
            kb.dma(SP, x_out[r0:r0 + 128, :], xr[s], reads=[Bxr[s]], writes=[Bxout])
        kb.barrier()


def host_masks():
    jl = np.arange(2)[:, None, None, None]
    kc = np.arange(64)[None, :, None, None]
    e = np.arange(16)[None, None, :, None]
    qc = np.arange(64)[None, None, None, :]
    dr = 15 - e + jl
    cs = np.clip(qc - 8, 0, 48)
    colok = (kc >= cs) & (kc < cs + 16)
    ok_edge = colok & (dr >= 0) & (dr <= 14)
    ok_int = colok & (dr >= 3) & (dr <= 10)
    m = np.stack([np.where(ok_edge, 0.0, -30000.0), np.where(ok_int, 0.0, -30000.0)]).astype(np.float32)
    return np.ascontiguousarray(np.broadcast_to(m, (2, 2, 64, 16, 64)).reshape(2, 128, 1024))


def host_gt(rpb):
    jl = np.arange(2)[:, None, None, None]
    kc = np.arange(64)[None, :, None, None]
    e = np.arange(16)[None, None, :, None]
    qc = np.arange(64)[None, None, None, :]
    dr = np.clip(15 - e + jl, 0, 14)
    dc = np.clip(kc - qc + 15, 0, 30)
    dr, dc = np.broadcast_arrays(dr, dc)
    return np.ascontiguousarray(rpb[:, dr, dc].reshape(8, 128, 1024).astype(np.float32))


def host_par(lam_re, lam_im, log_dt, b_re, b_im, c_re, c_im, d_skip, conv_w):
    par = np.zeros((128, NPAR), np.float32)

    def cols(a):
        return a.reshape(2, 8, 2, 64).transpose(2, 3, 0, 1).reshape(128, 16)
    par[:, PC_LR:PC_LR + 16] = cols(lam_re)
    par[:, PC_LI:PC_LI + 16] = cols(lam_im)
    par[:, PC_LDT:PC_LDT + 16] = cols(np.broadcast_to(log_dt[:, :, None], (2, 16, 64)))
    par[:, PC_D:PC_D + 2] = d_skip.reshape(2, 128).T
    par[:, PC_CW:PC_CW + 6] = conv_w.reshape(3, 2, 128).transpose(2, 1, 0).reshape(128, 6)
    def bcols(a):
        return a.reshape(2, 8, 2, 64, 16).transpose(2, 3, 0, 1, 4).reshape(128, 256)
    par[:, PC_BR:PC_BR + 256] = bcols(b_re)
    par[:, PC_BI:PC_BI + 256] = bcols(b_im)
    def ccols(a):
        return a.reshape(2, 8, 2, 16, 64).transpose(2, 4, 0, 1, 3).reshape(128, 256)
    par[:, PC_CR:PC_CR + 256] = ccols(c_re)
    par[:, PC_CI:PC_CI + 256] = ccols(c_im)
    return par


def build_ffn_prog(ntok=NTOK):
    nc = bass.Bass("TRN2", target_bir_lowering=False)
    x = nc.dram_tensor("x", [ntok, D], F32, kind="ExternalInput").ap()
    gpre = nc.dram_tensor("gpre", [D], F32, kind="ExternalInput").ap()
    gpost = nc.dram_tensor("gpost", [D], F32, kind="ExternalInput").ap()
    wg = nc.dram_tensor("wg", [D, DFF], F32, kind="ExternalInput").ap()
    wu = nc.dram_tensor("wu", [D, DFF], F32, kind="ExternalInput").ap()
    wd = nc.dram_tensor("wd", [DFF, D], F32, kind="ExternalInput").ap()
    ident = nc.dram_tensor("ident", [128, 128], F32, kind="ExternalInput").ap()
    y = nc.dram_tensor("y", [ntok, D], F32, kind="ExternalOutput").ap()
    with contextlib.ExitStack() as st:
        kb = KB(nc, st)
        C = Ctx(kb, ident)
        emit_ffn(kb, C, x, y, gpre, gpost, wg, wu, wd, Buf("x_in"), Buf("x_out"), ntok=ntok)
        kb.emit()
    return nc


MIX_W = [("gpre", [D]), ("gpost", [D]), ("w_in", [D, INCOLS]), ("par", [128, NPAR]), ("woa", [256, D]), ("wga", [256, D]),
         ("wgb", [256, D]), ("woc", [512, D]), ("wo", [D, D]), ("gt", [8, 128, 1024])]


def build_mixer_prog(nseq=2, dbg=False):
    nc = bass.Bass("TRN2", target_bir_lowering=False)
    ntok = nseq * SEQ
    x = nc.dram_tensor("x", [ntok, D], F32, kind="ExternalInput").ap()
    W = {}
    for name, shp in MIX_W:
        W[name] = nc.dram_tensor(name, shp, F32, kind="ExternalInput").ap()
    ident = nc.dram_tensor("ident", [128, 128], F32, kind="ExternalInput").ap()
    masks = nc.dram_tensor("masks", [2, 128, 1024], F32, kind="ExternalInput").ap()
    y = nc.dram_tensor("y", [ntok, D], F32, kind="ExternalOutput").ap()
    W["tabscr"] = nc.dram_tensor("tabscr", [16, 2, 128, SEQ], F32, kind="Internal").ap()
    d = None
    if dbg:
        d = {"ca": nc.dram_tensor("dbg_ca", [2, 128, SEQ], BF16, kind="ExternalOutput").ap(),
             "u": nc.dram_tensor("dbg_u", [2, 128, SEQ], F32, kind="ExternalOutput").ap(),
             "y": nc.dram_tensor("dbg_y", [2, 128, SEQ], F32, kind="ExternalOutput").ap(),
             "o": nc.dram_tensor("dbg_o", [SEQ, 512], BF16, kind="ExternalOutput").ap()}
    with contextlib.ExitStack() as st:
        kb = KB(nc, st)
        C = Ctx(kb, ident, masks)
        emit_mixer(kb, C, x, y, W, Buf("x_in"), Buf("x_out"), nseq=nseq, dbg=d)
        kb.emit()
    return nc


FFN_W = [("gpre", [D]), ("gpost", [D]), ("wg", [D, DFF]), ("wu", [D, DFF]), ("wd", [DFF, D])]
DEPTH = 4


def build_fused_prog(nseq=2, depth=DEPTH):
    nc = bass.Bass("TRN2", target_bir_lowering=False)
    ntok = nseq * SEQ
    x = nc.dram_tensor("x", [ntok, D], F32, kind="ExternalInput").ap()
    ident = nc.dram_tensor("ident", [128, 128], F32, kind="ExternalInput").ap()
    masks = nc.dram_tensor("masks", [2, 128, 1024], F32, kind="ExternalInput").ap()
    WL = []
    for L in range(depth):
        d = {}
        for blk, spec in (("f1", FFN_W), ("mx", MIX_W), ("f2", FFN_W)):
            d[blk] = {name: nc.dram_tensor("%s_%s_%d" % (blk, name, L), shp, F32, kind="ExternalInput").ap() for name, shp in spec}
        WL.append(d)
    y = nc.dram_tensor("y", [ntok, D], F32, kind="ExternalOutput").ap()
    scr = [nc.dram_tensor("xscr%d" % i, [ntok, D], F32, kind="Internal").ap() for i in range(2)]
    tabscr = nc.dram_tensor("tabscr", [16, 2, 128, SEQ], F32, kind="Internal").ap()
    for L in range(depth):
        WL[L]["mx"]["tabscr"] = tabscr
    with contextlib.ExitStack() as st:
        kb = KB(nc, st)
        C = Ctx(kb, ident, masks)
        cur, Bcur = x, Buf("x_in")
        nsub = 3 * depth
        k = 0
        for L in range(depth):
            for blk in ("f1", "mx", "f2"):
                k += 1
                dst = y if k == nsub else scr[k % 2]
                Bdst = Buf("xo%d" % k)
                w = WL[L][blk]
                if blk == "mx":
                    emit_mixer(kb, C, cur, dst, w, Bcur, Bdst, nseq=nseq)
                else:
                    emit_ffn(kb, C, cur, dst, w["gpre"], w["gpost"], w["wg"], w["wu"], w["wd"], Bcur, Bdst, ntok=ntok)
                cur, Bcur = dst, Bdst
        kb.emit()
    return nc


def _layer_inputs(inp, L):
    g = lambda k: np.ascontiguousarray(np.asarray(inp[k])[L], dtype=np.float32)
    m = {}
    f1 = {"gpre": g("norm_ffn1_pre"), "gpost": g("norm_ffn1_post"), "wg": g("ffn1_w_gate"), "wu": g("ffn1_w_up"),
          "wd": g("ffn1_w_down")}
    f2 = {"gpre": g("norm_ffn2_pre"), "gpost": g("norm_ffn2_post"), "wg": g("ffn2_w_gate"), "wu": g("ffn2_w_up"),
          "wd": g("ffn2_w_down")}
    mx = {"gpre": g("norm_mix_pre"), "gpost": g("norm_mix_post"), "w_in": g("w_in"),
          "par": host_par(g("ssm_lam_re"), g("ssm_lam_im"), g("ssm_log_dt"), g("ssm_b_re"), g("ssm_b_im"),
                          g("ssm_c_re"), g("ssm_c_im"), g("ssm_d"), g("conv_w")),
          "woa": g("w_out_a"), "wga": g("w_glu_a"), "wgb": g("w_glu_b"), "woc": g("w_out_c"), "wo": g("w_o"),
          "gt": host_gt(g("na_rpb"))}
    for blk, d in (("f1", f1), ("mx", mx), ("f2", f2)):
        for name, v in d.items():
            m["%s_%s_%d" % (blk, name, L)] = v
    return m


def kernel(**inp):
    x = np.ascontiguousarray(np.asarray(inp["x"], dtype=np.float32))
    B = x.shape[0]
    per = B // NCORES
    common = {"ident": np.eye(128, dtype=np.float32), "masks": host_masks()}
    for L in range(DEPTH):
        common.update(_layer_inputs(inp, L))
    nc = build_fused_prog(per)
    cores = list(range(NCORES))
    in_maps = [dict(common, x=np.ascontiguousarray(x[c * per:(c + 1) * per].reshape(per * SEQ, D))) for c in cores]
    res = run_bass_kernel_spmd(nc, in_maps, core_ids=cores)
    out = np.stack([np.asarray(res.results[c]["y"]).reshape(per, SEQ, D) for c in cores]).reshape(B, SEQ, D)
    return out.astype(np.float32)
```
